# Optimizing a Trainium2 kernel written in Bass

```python
import math
import jax, jax.numpy as jnp
from jax import lax
import numpy as np

D_MODEL = 1024
BATCH = 4
SEQ = 4096
DEPTH = 1

CHUNK = 64
N_META = 16
POOL_WIDTH = D_MODEL // 2
POOL_WINDOWS = (2, 4, 8, 16)
POOL_GROUPS = len(POOL_WINDOWS)
POOL_GROUP_DIM = POOL_WIDTH // POOL_GROUPS
N_HEADS = 8
HEAD_DIM = 64
ATTN_WIDTH = N_HEADS * 2 * HEAD_DIM
N_BUCKETS = 32
MAX_DISTANCE = 128
D_FF = ((-(-8 * D_MODEL // 3)) + 255) // 256 * 256
IN_COLS = POOL_WIDTH + 3 * ATTN_WIDTH + 2 * D_MODEL
QBLK = 128
NORM_EPS = 1e-6
NEG_INF = -1e30
BIG_CHUNK = 2 ** 30

kernel_name = "gated_pool_diffattn_hybrid_block"


def rmsnorm(x, w):
    xf = x.astype(jnp.float32)
    var = jnp.mean(xf * xf, axis=-1, keepdims=True)
    return (xf * lax.rsqrt(var + NORM_EPS) * w.astype(jnp.float32)).astype(x.dtype)


def t5_bucket(rel):
    nb = N_BUCKETS // 2
    ret = jnp.where(rel > 0, nb, 0)
    n = jnp.abs(rel)
    max_exact = nb // 2
    nf = jnp.maximum(n, max_exact).astype(jnp.float32)
    large = max_exact + (jnp.log(nf / max_exact) / math.log(MAX_DISTANCE / max_exact)
                         * (nb - max_exact)).astype(jnp.int32)
    large = jnp.minimum(large, nb - 1)
    return ret + jnp.where(n < max_exact, n, large)


def chunk_ids(pos, n_valid):
    cid = jnp.where(pos < N_META, 0, 1 + (pos - N_META) // CHUNK)
    return jnp.where(pos < n_valid, cid, BIG_CHUNK)


def pool_mixer(u, group_w, scale):
    B, L, C = u.shape
    uf = u.astype(jnp.float32)
    cs = jnp.concatenate([jnp.zeros((B, 1, C), jnp.float32), jnp.cumsum(uf, axis=1)], axis=1)
    t = jnp.arange(L)
    outs = []
    for g, w in enumerate(POOL_WINDOWS):
        sl = slice(g * POOL_GROUP_DIM, (g + 1) * POOL_GROUP_DIM)
        csg = cs[..., sl]
        lo = jnp.maximum(t + 1 - w, 0)
        cnt = (t + 1 - lo).astype(jnp.float32)
        mean = (csg[:, 1:] - csg[:, lo]) / cnt[None, :, None]
        outs.append(mean - uf[..., sl])
    pooled = jnp.stack(outs, axis=2).astype(u.dtype)
    mixed = jnp.einsum('blgc,gcd->blgd', pooled, group_w)
    return mixed.reshape(B, L, C) * scale


def diff_attention(q, k, v, bias_table, lam):
    B, L = q.shape[0], q.shape[1]
    Lp = -(-L // QBLK) * QBLK
    pad = Lp - L

    def to_bhld(a):
        a = jnp.pad(a, [(0, 0), (0, pad)] + [(0, 0)] * (a.ndim - 2))
        return jnp.swapaxes(a, 1, 2)

    q1, q2 = to_bhld(q[..., 0, :]), to_bhld(q[..., 1, :])
    k1, k2 = to_bhld(k[..., 0, :]), to_bhld(k[..., 1, :])
    vp = to_bhld(v)
    pos = jnp.arange(Lp)
    cid = chunk_ids(pos, L)
    nblk = Lp // QBLK
    scale = HEAD_DIM ** -0.5

    def blockify(a):
        return a.reshape(B, N_HEADS, nblk, QBLK, a.shape[-1]).transpose(2, 0, 1, 3, 4)

    def one_block(args):
        qa, qb, qpos = args
        rel = pos[None, :] - qpos[:, None]
        bias = jnp.transpose(bias_table[t5_bucket(rel)], (2, 0, 1)).astype(jnp.float32)
        visible = cid[None, :] <= chunk_ids(qpos, L)[:, None]

        def probs(qx, kx):
            s = jnp.einsum('bhqd,bhkd->bhqk', qx, kx).astype(jnp.float32) * scale + bias[None]
            s = jnp.where(visible[None, None], s, NEG_INF)
            return jax.nn.softmax(s, axis=-1)

        w = probs(qa, k1) - lam * probs(qb, k2)
        return jnp.einsum('bhqk,bhkd->bhqd', w.astype(vp.dtype), vp)

    o = lax.map(one_block, (blockify(q1), blockify(q2), pos.reshape(nblk, QBLK)))
    o = o.transpose(1, 0, 3, 2, 4).reshape(B, Lp, N_HEADS, 2 * HEAD_DIM)
    return o[:, :L]


def hybrid_layer(h, layer_idx, bias_table, mix_norm_w, w_in, pool_group_w, pool_scale,
                 lambda_q1, lambda_k1, lambda_q2, lambda_k2, subln_w,
                 w_pool_out, w_attn_out, w_o, ffn_norm_w, w_gate, w_up, w_down):
    B, L, _ = h.shape
    xn = rmsnorm(h, mix_norm_w)
    proj = xn @ w_in
    offs = np.cumsum([POOL_WIDTH, ATTN_WIDTH, ATTN_WIDTH, ATTN_WIDTH, D_MODEL]).tolist()
    u_pool, q, k, v, g_pool, g_attn = jnp.split(proj, offs, axis=-1)

    pool_out = pool_mixer(u_pool, pool_group_w, pool_scale)

    lam_init = 0.8 - 0.6 * math.exp(-0.3 * layer_idx)
    lam = (jnp.exp(jnp.sum(lambda_q1.astype(jnp.float32) * lambda_k1.astype(jnp.float32)))
           - jnp.exp(jnp.sum(lambda_q2.astype(jnp.float32) * lambda_k2.astype(jnp.float32)))
           + lam_init)
    o = diff_attention(q.reshape(B, L, N_HEADS, 2, HEAD_DIM),
                       k.reshape(B, L, N_HEADS, 2, HEAD_DIM),
                       v.reshape(B, L, N_HEADS, 2 * HEAD_DIM), bias_table, lam)
    o = rmsnorm(o, subln_w) * (1.0 - lam_init)
    attn_out = o.reshape(B, L, ATTN_WIDTH)

    merged = (jax.nn.sigmoid(g_pool) * (pool_out @ w_pool_out)
              + jax.nn.sigmoid(g_attn) * (attn_out @ w_attn_out))
    h = h + merged @ w_o

    hn = rmsnorm(h, ffn_norm_w)
    h = h + (jax.nn.silu(hn @ w_gate) * (hn @ w_up)) @ w_down
    return h


def setup_inputs(seed: int = 0) -> dict:
    key = jax.random.key(seed)
    ks = jax.random.split(key, 24)
    f32 = jnp.float32

    def nrm(k, shape, scale):
        return jax.random.normal(k, shape, f32) * scale

    return {
        "x": nrm(ks[0], (BATCH, SEQ, D_MODEL), 1.0),
        "meta_tokens": nrm(ks[1], (N_META, D_MODEL), 1.0),
        "rel_bias_table": nrm(ks[2], (N_BUCKETS, N_HEADS), 0.5),
        "mix_norm_w": 1.0 + nrm(ks[3], (DEPTH, D_MODEL), 0.02),
        "w_in": nrm(ks[4], (DEPTH, D_MODEL, IN_COLS), D_MODEL ** -0.5),
        "pool_group_w": nrm(ks[5], (DEPTH, POOL_GROUPS, POOL_GROUP_DIM, POOL_GROUP_DIM), POOL_GROUP_DIM ** -0.5),
        "pool_scale": 1.0 + nrm(ks[6], (DEPTH, POOL_WIDTH), 0.02),
        "lambda_q1": nrm(ks[7], (DEPTH, HEAD_DIM), 0.1),
        "lambda_k1": nrm(ks[8], (DEPTH, HEAD_DIM), 0.1),
        "lambda_q2": nrm(ks[9], (DEPTH, HEAD_DIM), 0.1),
        "lambda_k2": nrm(ks[10], (DEPTH, HEAD_DIM), 0.1),
        "subln_w": 1.0 + nrm(ks[11], (DEPTH, 2 * HEAD_DIM), 0.02),
        "w_pool_out": nrm(ks[12], (DEPTH, POOL_WIDTH, D_MODEL), POOL_WIDTH ** -0.5),
        "w_attn_out": nrm(ks[13], (DEPTH, ATTN_WIDTH, D_MODEL), ATTN_WIDTH ** -0.5),
        "w_o": nrm(ks[14], (DEPTH, D_MODEL, D_MODEL), D_MODEL ** -0.5),
        "ffn_norm_w": 1.0 + nrm(ks[15], (DEPTH, D_MODEL), 0.02),
        "w_gate": nrm(ks[16], (DEPTH, D_MODEL, D_FF), D_MODEL ** -0.5),
        "w_up": nrm(ks[17], (DEPTH, D_MODEL, D_FF), D_MODEL ** -0.5),
        "w_down": nrm(ks[18], (DEPTH, D_FF, D_MODEL), D_FF ** -0.5),
        "final_norm_w": 1.0 + nrm(ks[19], (D_MODEL,), 0.02),
    }


def reference(x, meta_tokens, rel_bias_table, mix_norm_w, w_in, pool_group_w, pool_scale,
              lambda_q1, lambda_k1, lambda_q2, lambda_k2, subln_w, w_pool_out, w_attn_out,
              w_o, ffn_norm_w, w_gate, w_up, w_down, final_norm_w):
    B = x.shape[0]
    meta = jnp.broadcast_to(meta_tokens[None].astype(x.dtype), (B, N_META, x.shape[-1]))
    h = jnp.concatenate([meta, x], axis=1)
    for i in range(DEPTH):
        h = hybrid_layer(h, i, rel_bias_table, mix_norm_w[i], w_in[i], pool_group_w[i], pool_scale[i],
                         lambda_q1[i], lambda_k1[i], lambda_q2[i], lambda_k2[i], subln_w[i],
                         w_pool_out[i], w_attn_out[i], w_o[i], ffn_norm_w[i],
                         w_gate[i], w_up[i], w_down[i])
    h = rmsnorm(h, final_norm_w)
    return h[:, N_META:]
```

```python
import math
import numpy as np
from contextlib import ExitStack
import concourse.bass as bass
import concourse.mybir as mybir
from concourse.bass_utils import run_bass_kernel_spmd

F32 = mybir.dt.float32
BF16 = mybir.dt.bfloat16
ALU = mybir.AluOpType
AF = mybir.ActivationFunctionType
AX = mybir.AxisListType

NSLOT = 33
NOWN = 16
EPS = 1e-6
DFF = 2816
NJ = DFF // 128
DBG = {}


class Ctx:
    ENG = ("pe", "act", "dve", "pool", "sp")

    def __init__(self, nc, es):
        self.nc, self.es = nc, es
        self.ops = {e: [] for e in self.ENG}
        self.sem = {}
        self.val = {}
        self.waited = {}
        self.lastw = {}
        self.readers = {}
        self.maxwait = {}
        self.dead = False

    def _sem(self, key):
        if key not in self.sem:
            self.sem[key] = self.es.enter_context(self.nc.semaphore("s%d" % len(self.sem)))
            self.val[key] = 0
        return self.sem[key]

    def _deps(self, eng, reads, writes, extra=()):
        toks = list(extra)
        for k in reads:
            if k in self.lastw:
                toks.append(self.lastw[k])
        for k in writes:
            if k in self.lastw:
                toks.append(self.lastw[k])
            toks.extend(self.readers.get(k, {}).items())
        need = {}
        for (sk, v) in toks:
            if sk == "E_pe" and eng == "pe":
                continue
            if self.waited.get((eng, sk), 0) >= v:
                continue
            need[sk] = max(need.get(sk, 0), v)
        for sk, v in need.items():
            self.waited[(eng, sk)] = v
            self.maxwait[sk] = max(self.maxwait.get(sk, 0), v)
        return list(need.items())

    def _commit(self, tok, reads, writes):
        for k in reads:
            d = self.readers.setdefault(k, {})
            d[tok[0]] = max(d.get(tok[0], 0), tok[1])
        for k in writes:
            self.lastw[k] = tok
            self.readers[k] = {}

    def op(self, eng, fn, reads=(), writes=(), inc=True, extra=()):
        if self.dead:
            return None
        waits = self._deps(eng, reads, writes, extra)
        sk = "E_" + eng
        self._sem(sk)
        tok = (sk, self.val[sk] + 1)
        if inc:
            self.val[sk] += 1
        self.ops[eng].append((fn, waits, (sk, 1) if inc else None))
        self._commit(tok, reads, writes)
        return tok

    def dma(self, eng, fn, semkey, reads=(), writes=(), extra=()):
        if self.dead:
            return None
        waits = self._deps(eng, reads, writes, extra)
        self._sem(semkey)
        self.val[semkey] += 16
        tok = (semkey, self.val[semkey])
        self.ops[eng].append((fn, waits, (semkey, 16)))
        self._commit(tok, reads, writes)
        return tok

    def wait_only(self, eng, toks):
        if self.dead:
            return
        waits = self._deps(eng, (), (), toks)
        self.ops[eng].append((None, waits, None))

    def barrier(self):
        if self.dead:
            return
        toks = [(sk, v) for sk, v in self.val.items() if v > 0]
        for e in self.ENG:
            self.wait_only(e, toks)
        self.lastw = {}
        self.readers = {}

    def emit(self):
        for sk, v in self.maxwait.items():
            assert v <= self.val[sk], ("wait beyond final count", sk, v, self.val[sk])
        nc = self.nc
        with nc.Block() as block:
            def mk(name):
                def run(e):
                    for fn, waits, inc in self.ops[name]:
                        for sk, v in waits:
                            e.wait_ge(self.sem[sk], v)
                        if fn is None:
                            continue
                        ins = fn(e)
                        if inc is not None:
                            ins.then_inc(self.sem[inc[0]], inc[1])
                return run
            block.tensor(mk("pe"))
            block.scalar(mk("act"))
            block.vector(mk("dve"))
            block.gpsimd(mk("pool"))
            block.sync(mk("sp"))


class Arena:
    def __init__(self, t, nbytes):
        self.t, self.nbytes, self.off = t, nbytes, 0
        self.holes = []

    def alloc(self, shape, dt):
        n = int(np.prod(shape))
        sz = 4 if dt == F32 else 2
        self.off = (self.off + 3) // 4 * 4
        for lo, hi in self.holes:
            if self.off < hi and self.off + n * sz > lo:
                self.off = hi
        a, b = self.off // 2, (self.off + n * sz) // 2
        self.off += n * sz
        assert self.off <= self.nbytes, ("arena overflow", self.off, self.nbytes)
        self.peak = max(getattr(self, 'peak', 0), self.off)
        ap = self.t[:, a:b]
        if dt == F32:
            ap = ap.bitcast(F32)
        if len(shape) == 2:
            ap = ap.rearrange("p (a b) -> p a b", a=shape[0])
        elif len(shape) == 3:
            ap = ap.rearrange("p (a b c) -> p a b c", a=shape[0], b=shape[1])
        return ap


class Rot:
    def __init__(self, items):
        self.items, self.i = items, 0

    def next(self):
        it = self.items[self.i % len(self.items)]
        self.i += 1
        return it


def build(debug=0, upto=0):
    nc = bass.Bass("TRN2", target_bir_lowering=False)

    def din(name, shape, dt=F32):
        return nc.dram_tensor(name, shape, dt, kind="ExternalInput").ap()

    xk = din("xk", [NSLOT * 128, 1024])
    xo = din("xo", [NOWN, 144, 1024])
    w_in = din("w_in", [1024, 5632])
    pgw = din("pgw", [4, 128, 128])
    w_po = din("w_po", [512, 1024])
    w_ao = din("w_ao", [1024, 1024])
    w_o = din("w_o", [1024, 1024])
    w_g = din("w_g", [1024, DFF])
    w_u = din("w_u", [1024, DFF])
    w_d = din("w_d", [DFF, 1024])
    normw = din("normw", [3, 128, 1024])
    sublnw = din("sublnw", [128, 128])
    lamv = din("lamv", [128, 256])
    cfar_d = din("cfar", [128, 8])
    kvalid_d = din("kvalid", [128, NSLOT])
    psc_d = din("psc", [128, 4])
    ident_d = din("ident", [128, 128])
    maskadd_d = din("maskadd", [128, 128])
    biasT_d = din("biasT", [128, 2 * 8 * 128])
    out = nc.dram_tensor("out", [NOWN * 128, 1024], F32, kind="ExternalOutput").ap()
    xnd = nc.dram_tensor("xnd", [NSLOT + NOWN, 128, 8, 128], BF16).ap()
    h1d = nc.dram_tensor("h1d", [NOWN * 128, 1024], F32).ap()
    dbg = {}
    if debug:
        for nm, shp, dt in (("d_kt", [128, 4 * NSLOT * 128], BF16), ("d_vx", [128, NSLOT * 4 * 129], BF16),
                            ("d_qt", [128, 4 * 2048], BF16), ("d_attnT", [128, 8 * 2048], BF16),
                            ("d_h1", [NOWN * 128, 1024], F32)):
            dbg[nm] = nc.dram_tensor(nm, shp, dt, kind="ExternalOutput").ap()

    ARENA_BYTES = 212000
    with ExitStack() as es:
        arena_t = es.enter_context(nc.sbuf_tensor("arena", [128, ARENA_BYTES // 2], BF16))
        pall_t = es.enter_context(nc.psum_tensor("pall", [128, 4096], F32))
        pall = pall_t[:]
        ps = [pall[:, i * 512:(i + 1) * 512] for i in range(8)]
        ctx = Ctx(nc, es)
        A = Arena(arena_t, ARENA_BYTES)

        def stop(n):
            if upto == n and not ctx.dead:
                ctx.barrier()
                ctx.dead = True

        idf = A.alloc((128,), F32)
        idb = A.alloc((128,), BF16)
        nhalf = A.alloc((4,), F32)
        lam_sb = A.alloc((2,), F32)
        cfar = A.alloc((8,), F32)
        ncfar = A.alloc((8,), F32)
        kvalid = A.alloc((NSLOT,), F32)
        psc = A.alloc((4,), F32)
        subw = A.alloc((128,), F32)
        nw = A.alloc((1024,), F32)
        E = A.alloc((2, 8, 128), BF16)
        ssb = [A.alloc((4,), F32) for _ in range(3)]
        varb = [A.alloc((4,), F32) for _ in range(3)]
        rstdb = [A.alloc((4,), F32) for _ in range(3)]
        pss = [A.alloc((2,), F32) for _ in range(4)]
        pvar = [A.alloc((2,), F32) for _ in range(4)]
        prs = [A.alloc((2,), F32) for _ in range(4)]
        junk = A.alloc((1024,), BF16)
        xh = [A.alloc((1024,), BF16) for _ in range(2)]
        xt = [A.alloc((1024,), F32) for _ in range(2)]
        base_mark = A.off

        def cdma(dst, src, key):
            ctx.dma("sp", lambda e: e.dma_start(out=dst, in_=src), "c_" + key, writes=[key])

        cdma(idf, ident_d, "idf")
        cdma(cfar, cfar_d, "cfar")
        cdma(kvalid, kvalid_d, "kvalid")
        cdma(psc, psc_d, "psc")
        cdma(subw, sublnw, "subw0")
        cdma(nw, normw[0], "nw")
        ctx.op("dve", lambda e: e.tensor_copy(out=idb, in_=idf), reads=["idf"], writes=["idb"])
        ctx.op("pool", lambda e: e.memset(nhalf, -0.5), writes=["nhalf"])
        ctx.op("dve", lambda e: e.tensor_scalar(out=ncfar, in0=cfar, scalar1=-1.0, scalar2=None, op0=ALU.mult),
               reads=["cfar"], writes=["ncfar"])
        ctx.op("dve", lambda e: e.tensor_scalar(out=subw, in0=subw, scalar1=0.8, scalar2=None, op0=ALU.mult),
               reads=["subw0"], writes=["subw"])
        m0 = A.off
        lv = A.alloc((256,), F32)
        lp = A.alloc((128,), F32)
        lsum = A.alloc((2,), F32)
        lexp = A.alloc((2,), F32)
        bT = A.alloc((2, 8, 128), F32)
        madd = A.alloc((128,), F32)
        def setup_lam_E():
            cdma(lv, lamv, "lv")
            cdma(bT.rearrange("p a b c -> p (a b c)"), biasT_d, "bT")
            cdma(madd, maskadd_d, "madd")
            ctx.op("dve", lambda e: e.tensor_tensor(out=lp.rearrange("p (a b) -> p a b", a=2),
                                                    in0=lv.rearrange("p (a b c) -> p a b c", a=2, b=2)[:, :, 0, :],
                                                    in1=lv.rearrange("p (a b c) -> p a b c", a=2, b=2)[:, :, 1, :], op=ALU.mult),
                   reads=["lv"], writes=["lp"])
            ctx.op("dve", lambda e: e.tensor_reduce(out=lsum, in_=lp.rearrange("p (a b) -> p a b", a=2), axis=AX.X, op=ALU.add),
                   reads=["lp"], writes=["lsum"])
            ctx.op("act", lambda e: e.activation(out=lexp, in_=lsum, func=AF.Exp), reads=["lsum"], writes=["lexp"])
            ctx.op("dve", lambda e: e.tensor_tensor(out=lam_sb[:, 0:1], in0=lexp[:, 0:1], in1=lexp[:, 1:2], op=ALU.subtract),
                   reads=["lexp"], writes=["lam0"])
            ctx.op("dve", lambda e: e.tensor_scalar(out=lam_sb[:, 0:1], in0=lam_sb[:, 0:1], scalar1=0.2, scalar2=None, op0=ALU.add),
                   reads=["lam0"], writes=["lam1"])
            ctx.op("dve", lambda e: e.tensor_scalar(out=lam_sb[:, 1:2], in0=lam_sb[:, 0:1], scalar1=-1.0, scalar2=None, op0=ALU.mult),
                   reads=["lam1"], writes=["lam"])
            ctx.op("dve", lambda e: e.tensor_tensor(out=bT[:, 0, :, :], in0=bT[:, 0, :, :],
                                                    in1=madd.unsqueeze(1).to_broadcast([128, 8, 128]), op=ALU.add),
                   reads=["bT", "madd"], writes=["bT2"])
            for ty in range(2):
                for hh in range(8):
                    ctx.op("act", lambda e, ty=ty, hh=hh: e.activation(out=E[:, ty, hh, :], in_=bT[:, ty, hh, :], func=AF.Exp,
                                                                       bias=ncfar[:, hh:hh + 1], scale=1.0),
                           reads=["bT2", "ncfar"], writes=[("E", ty, hh)])

        A.off = m0
        stop(1)

        rot3 = Rot([0, 1, 2])
        rot_xh = Rot([0, 1])
        rot_pT = Rot([0, 1])
        evac_flip = [0]

        def evac(out_ap, in_ap, reads, writes, eng=None):
            evac_flip[0] ^= 1
            if (evac_flip[0] and eng is None) or eng == "act":
                return ctx.op("act", lambda e: e.activation(out=out_ap, in_=in_ap, func=AF.Copy), reads=reads, writes=writes)
            return ctx.op("dve", lambda e: e.tensor_copy(out=out_ap, in_=in_ap), reads=reads, writes=writes)

        def rstd_of(ss_ap, ss_key, n, k, inv_n):
            b = rot3.next()
            var, rs = varb[b], rstdb[b]
            ctx.op("pool", lambda e: e.tensor_scalar(out=var[:n, :k], in0=ss_ap, scalar1=inv_n, scalar2=EPS, op0=ALU.mult, op1=ALU.add),
                   reads=[ss_key], writes=[("var", b)])
            ctx.op("pool", lambda e: e.tensor_tensor(out=rs[:n, :k], in0=var[:n, :k], in1=nhalf[:n, :k], op=ALU.pow),
                   reads=[("var", b), "nhalf"], writes=[("rstd", b)])
            return rs[:n, :k], ("rstd", b)

        def norm_T(src, src_key, n, dst, dst_key):
            b = rot3.next()
            ss = ssb[b]
            ctx.op("act", lambda e: e.activation(out=junk[:n, :], in_=src, func=AF.Square, accum_out=ss[:n, 0:1]),
                   reads=[src_key], writes=["junk", ("ss", b)])
            rs, rkey = rstd_of(ss[:n, 0:1], ("ss", b), n, 1, 1.0 / 1024)
            hb = rot_xh.next()
            ctx.op("dve", lambda e: e.scalar_tensor_tensor(out=xh[hb][:n, :], in0=src, scalar=rs, in1=nw[:n, :], op0=ALU.mult, op1=ALU.mult),
                   reads=[src_key, rkey, "nw"], writes=[("xh", hb)])
            pb = rot_pT.next()
            pT = ps[pb][:].bitcast(BF16)
            for c in range(8):
                ctx.op("pe", lambda e, c=c: e.transpose(out=pT[:, c * 128:c * 128 + n], in_=xh[hb][:n, c * 128:(c + 1) * 128], identity=idb[:n, :n]),
                       reads=[("xh", hb), "idb"], writes=[("ps", pb)], inc=(c == 7))
            evac(dst, pT.rearrange("p (c t) -> p c t", c=8)[:, :, :n], [("ps", pb)], [dst_key])

        def norm_pipe(jobs, consume, s2off=2, xhb=None, evac_eng=None):
            xhb = xhb or xh
            nxh = len(xhb)
            N = len(jobs)

            def S0(k):
                if jobs[k].get("load") is not None:
                    jobs[k]["load"]()

            def S1(k):
                j = jobs[k]
                n, b = j["n"], k % 4
                src_, ss = j["src"], pss[b]
                ctx.op("act", lambda e: e.activation(out=junk[:n, :], in_=src_, func=AF.Square, accum_out=ss[:n, 0:1]),
                       reads=[j["src_key"]], writes=["junk", ("pss", b)])
                ctx.op("pool", lambda e: e.tensor_scalar(out=pvar[b][:n, 0:1], in0=ss[:n, 0:1], scalar1=1.0 / 1024, scalar2=EPS, op0=ALU.mult, op1=ALU.add),
                       reads=[("pss", b)], writes=[("pvar", b)])
                ctx.op("pool", lambda e: e.tensor_tensor(out=prs[b][:n, 0:1], in0=pvar[b][:n, 0:1], in1=nhalf[:n, 0:1], op=ALU.pow),
                       reads=[("pvar", b), "nhalf"], writes=[("prs", b)])

            def S2(k):
                j = jobs[k]
                n, b, hb = j["n"], k % 4, k % nxh
                src_ = j["src"]
                ctx.op("dve", lambda e: e.scalar_tensor_tensor(out=xhb[hb][:n, :], in0=src_, scalar=prs[b][:n, 0:1], in1=nw[:n, :], op0=ALU.mult, op1=ALU.mult),
                       reads=[j["src_key"], ("prs", b), "nw"], writes=[("xh", hb)])

            def S3(k):
                j = jobs[k]
                n, hb = j["n"], k % nxh
                pb = rot_pT.next()
                pT = ps[pb][:].bitcast(BF16)
                for c in range(8):
                    ctx.op("pe", lambda e, c=c: e.transpose(out=pT[:, c * 128:c * 128 + n], in_=xhb[hb][:n, c * 128:(c + 1) * 128], identity=idb[:n, :n]),
                           reads=[("xh", hb), "idb"], writes=[("ps", pb)], inc=(c == 7))
                evac(j["dst"], pT.rearrange("p (c t) -> p c t", c=8)[:, :, :n], [("ps", pb)], [j["dst_key"]], eng=evac_eng)

            for it in range(-(s2off + 2), N):
                for stage, off in ((S0, s2off + 2), (S1, s2off + 1), (S2, s2off), (S3, 1)):
                    if 0 <= it + off < N:
                        stage(it + off)
                if it >= 0:
                    consume(it)

        attn_all = A.alloc((NOWN, 1024), BF16)
        pass_mark = A.off
        rot_mm = Rot([2, 3, 4, 5])
        out_stores = []

        for hp in range(2):
            A.off = pass_mark
            KT = A.alloc((4, NSLOT * 128), BF16)
            Vx = A.alloc((NSLOT, 4, 129), BF16)
            QT = A.alloc((4, 2048), BF16)
            off_dead = (A.off + 3) // 4 * 4
            wq = A.alloc((8, 512), BF16)
            wk = A.alloc((8, 512), BF16)
            wv = A.alloc((8, 512), BF16)
            xnTg = [A.alloc((8, 512), BF16) for _ in range(3)]
            xtp = [xt[0], xt[1], A.alloc((1024,), F32)]
            PT = [A.alloc((2, 512), BF16) for _ in range(3)]
            Ost = A.alloc((8, 129), F32)
            o1 = A.alloc((4, 128), F32)
            o2 = A.alloc((4, 128), F32)
            rz = A.alloc((8,), F32)

            w_in_v = w_in.rearrange("(c p) n -> p c n", p=128)
            def load_qkv_w(hpx, defer=None):
                for nm, dst, c0 in (("wv", wv, 2560), ("wk", wk, 1536), ("wq", wq, 512)):
                    cs = slice(c0 + hpx * 512, c0 + (hpx + 1) * 512)

                    def issue(nm=nm, dst=dst, cs=cs):
                        ctx.dma("pool", lambda e: e.dma_start(out=dst, in_=w_in_v[:, :, cs]), "w_" + nm, writes=[nm])
                    if defer is None:
                        issue()
                    else:
                        defer.append(issue)
            w0_defer = []
            if hp == 0:
                load_qkv_w(0, w0_defer)
                w0_defer.pop(0)()
            for h in range(4):
                ctx.op("dve", lambda e, h=h: e.tensor_copy(out=Vx[:, :, h, 128], in_=kvalid), reads=["kvalid"], writes=[("Vx1", h)])

            seq = []
            for sg in range(9):
                tl = list(range(4 * sg, min(4 * sg + 4, NSLOT)))
                for j, t in enumerate(tl):
                    seq.append(("k", sg, j, t, len(tl)))
            for og in range(4):
                for j in range(4):
                    seq.append(("q", 9 + og, j, 4 * og + j, 4))

            def compute(n):
                kind, g, j, t, nt = seq[n]
                gb = g % 3
                if w0_defer and n in (1, 8):
                    w0_defer.pop(0)()
                if kind == "k":
                    pb = rot_mm.next()
                    for c in range(8):
                        ctx.op("pe", lambda e, c=c: e.matmul(ps[pb][:], lhsT=xnTg[gb][:, c, j * 128:(j + 1) * 128], rhs=wv[:, c, :],
                                                            start=(c == 0), stop=(c == 7)),
                               reads=[("xnTg", gb, j), "wv"], writes=[("ps", pb)], inc=(c == 7))
                    evac(Vx[:, t, :, 0:128], ps[pb][:].rearrange("p (h d) -> p h d", h=4), [("ps", pb)], [("Vx", t)])
                if j != nt - 1:
                    return
                ntok = 128 * nt
                for h in range(4):
                    pbk = rot_mm.next()
                    wsrc, wkey = (wk, "wk") if kind == "k" else (wq, "wq")
                    for c in range(8):
                        ctx.op("pe", lambda e, c=c, h=h, pbk=pbk, wsrc=wsrc: e.matmul(ps[pbk][:, 0:ntok], lhsT=wsrc[:, c, h * 128:(h + 1) * 128], rhs=xnTg[gb][:, c, 0:ntok],
                                                                                 start=(c == 0), stop=(c == 7)),
                               reads=[("xnTg", gb, jj) for jj in range(nt)] + [wkey], writes=[("ps", pbk)], inc=(c == 7))
                    if kind == "k":
                        evac(KT[:, h, g * 512:g * 512 + ntok], ps[pbk][:, 0:ntok], [("ps", pbk)], [("KT", h, g)])
                    else:
                        og_ = g - 9
                        evac(QT[:, h, og_ * 512:(og_ + 1) * 512], ps[pbk][:], [("ps", pbk)], [("QT", h, og_)])

            jobs = []
            for n, (kind, g, j, t, nt) in enumerate(seq):
                xb = n % 3
                src_ap = xk[t * 128:(t + 1) * 128, :] if kind == "k" else xo[t, 16:144, :]
                xdst = xtp[xb]

                def ld(xdst=xdst, src_ap=src_ap, xb=xb):
                    ctx.dma("sp", lambda e: e.dma_start(out=xdst, in_=src_ap), "x_%d" % xb, writes=[("xt", xb)])
                jobs.append(dict(load=ld, src=xdst, src_key=("xt", xb), n=128,
                                 dst=xnTg[g % 3][:, :, j * 128:(j + 1) * 128], dst_key=("xnTg", g % 3, j)))
            xtok = {}
            if hp == 0:
                def compute_and_save(n):
                    kind, g, j, t, nt = seq[n]
                    gb = g % 3
                    tsrc = xnTg[gb][:, :, j * 128:(j + 1) * 128]
                    sk_ = "xs_%d" % (n % 4)
                    xtok[sk_] = ctx.dma("sp", lambda e: e.dma_start(out=xnd[n], in_=tsrc), sk_, reads=[("xnTg", gb, j)], writes=[("xnd", n)],
                                        extra=[xtok[sk_]] if sk_ in xtok else [])
                    compute(n)
                norm_pipe(jobs, compute_and_save)
            else:
                def load_xn(n):
                    kind, g, j, t, nt = seq[n]
                    gb = g % 3
                    tdst = xnTg[gb][:, :, j * 128:(j + 1) * 128]
                    sk_ = "xl_%d" % (n % 6)
                    xtok[sk_] = ctx.dma("sp", lambda e: e.dma_start(out=tdst, in_=xnd[n]), sk_, reads=[("xnd", n)], writes=[("xnTg", gb, j)],
                                        extra=[xtok[sk_]] if sk_ in xtok else [])
                PFX = 5
                for n in range(min(PFX, len(seq))):
                    load_xn(n)
                for n in range(len(seq)):
                    if n + PFX < len(seq):
                        load_xn(n + PFX)
                    compute(n)
            if hp == 0:
                setup_lam_E()
            pending = []
            if hp == 0:
                load_qkv_w(1, pending)
            if hp == 1:
                save_off = A.off
                A.off = off_dead
                wu_ = A.alloc((8, 512), BF16)
                wgt = A.alloc((8, 2048), BF16)
                gw = A.alloc((4, 128), BF16)
                wpo = A.alloc((4, 1024), BF16)
                dead_end = A.off
                assert dead_end <= off_dead + 3 * 8192 + 3 * 8192 + 4096
                A.off = save_off
                deadk = ["wq", "wk", "wv", ("xt", 2)] + [("xnTg", g_, j_) for g_ in range(3) for j_ in range(4)]
                pending.append(lambda: ctx.dma("pool", lambda e: e.dma_start(out=wu_, in_=w_in_v[:, :, 0:512]), "w_wu", writes=["wu"] + deadk))
                pending.append(lambda: ctx.dma("pool", lambda e: e.dma_start(out=gw, in_=pgw.rearrange("g c d -> c g d")), "w_gw", writes=["gw"] + deadk))
                for q in range(4):
                    pending.append(lambda q=q: ctx.dma("pool", lambda e: e.dma_start(out=wgt[:, :, q * 512:(q + 1) * 512], in_=w_in_v[:, :, 3584 + q * 512:3584 + (q + 1) * 512]),
                                                      "w_wgt%d" % q, writes=[("wgt", q)] + deadk))
                pending.append(lambda: ctx.dma("pool", lambda e: e.dma_start(out=wpo, in_=w_po.rearrange("(g p) n -> p g n", p=128)), "w_wpo", writes=["wpo"] + deadk))

            if debug and hp == 0:
                ctx.dma("sp", lambda e: e.dma_start(out=dbg["d_kt"], in_=KT.rearrange("p a b -> p (a b)")), "dbg",
                        reads=[("KT", h, sg) for h in range(4) for sg in range(9)])
                ctx.dma("sp", lambda e: e.dma_start(out=dbg["d_vx"], in_=Vx.rearrange("p a b c -> p (a b c)")), "dbg",
                        reads=[("Vx", t) for t in range(NSLOT)] + [("Vx1", h) for h in range(4)])
                ctx.dma("sp", lambda e: e.dma_start(out=dbg["d_qt"], in_=QT.rearrange("p a b -> p (a b)")), "dbg",
                        reads=[("QT", h, og) for h in range(4) for og in range(4)])

            if hp == 0:
                stop(2)
            rot_s = Rot([0, 2])
            rot_pt = Rot([0, 1, 2])
            units = [(og, h) for og in range(4) for h in range(4)]
            items = []
            for ui, (og, h) in enumerate(units):
                i0 = 4 * og
                for s in range(2 * i0 + 9):
                    items.append((ui, s))

            def acc_bank(ui, bi):
                return 4 + ((-ui) % 4 + bi) % 4

            def acc_ap(ui, a):
                bank = acc_bank(ui, a // 3)
                c0 = (a % 3) * 130
                return ps[bank][:, c0:c0 + 129], ("ps", bank)

            sbank = {}

            def emit_qk(k):
                ui, s = items[k]
                og, h = units[ui]
                i0 = 4 * og
                jmin = max(0, (s - 2 * i0 - 2 + 1) // 2)
                ncol = (4 - jmin) * 128
                sb_ = rot_s.next()
                sbank[k] = sb_
                q0 = og * 512 + jmin * 128
                for m in range(2):
                    ctx.op("pe", lambda e, m=m: e.matmul(ps[sb_ + m][:, 0:ncol], lhsT=KT[m * 64:(m + 1) * 64, h, s * 128:(s + 1) * 128],
                                                         rhs=QT[m * 64:(m + 1) * 64, h, q0:q0 + ncol], start=True, stop=True),
                           reads=[("KT", h, s // 4), ("QT", h, og)], writes=[("ps", sb_ + m)], inc=(m == 1))

            def emit_rest(k):
                ui, s = items[k]
                og, h = units[ui]
                hh = 4 * hp + h
                i0 = 4 * og
                nsl = 2 * i0 + 9
                jmin = max(0, (s - 2 * i0 - 2 + 1) // 2)
                ncol = (4 - jmin) * 128
                sb_ = sbank.pop(k)
                pb = rot_pt.next()
                s_in = pall[:, sb_ * 512:(sb_ + 2) * 512].rearrange("p (m c) -> p m c", m=2)[:, :, 0:ncol]
                ctx.op("act", lambda e: e.activation(out=PT[pb][:, :, 0:ncol], in_=s_in, func=AF.Exp,
                                                     bias=cfar[:, hh:hh + 1], scale=0.125),
                       reads=[("ps", sb_), ("ps", sb_ + 1), "cfar"], writes=[("PT", pb, jj) for jj in range(4)])
                d = s - (2 * i0 + 2)
                if d >= -1:
                    if d % 2 == 0:
                        jf, ty = d // 2, 0
                    else:
                        jf, ty = (d + 1) // 2, 1
                    cf = (jf - jmin) * 128
                    ctx.op("dve", lambda e: e.tensor_tensor(out=PT[pb][:, :, cf:cf + 128], in0=PT[pb][:, :, cf:cf + 128],
                                                             in1=E[:, ty, hh, :].unsqueeze(1).to_broadcast([128, 2, 128]), op=ALU.mult),
                           reads=[("PT", pb, jf - jmin), ("E", ty, hh)], writes=[("PT", pb, jf - jmin)])
                jfix = jf if d >= -1 else -1
                pv = [(m, j) for m in range(2) for j in range(jmin, 4) if j != jfix] + [(m, j) for m in range(2) for j in range(jmin, 4) if j == jfix]
                for idx, (m, j) in enumerate(pv):
                    a = m * 4 + j
                    acc, akey = acc_ap(ui, a)
                    first = (s == 0 and a % 3 == 0)
                    last = (s == 2 * (i0 + j) + 2)
                    cj = (j - jmin) * 128
                    ctx.op("pe", lambda e, acc=acc, cj=cj, m=m, first=first, last=last: e.matmul(acc, lhsT=PT[pb][:, m, cj:cj + 128], rhs=Vx[:, s, h, :],
                                                                                           start=first, stop=last, skip_group_check=True),
                           reads=[("PT", pb, j - jmin), ("Vx", s), ("Vx1", h)], writes=[akey], inc=(idx == len(pv) - 1))
                if s == nsl - 1:
                    for bi in range(3):
                        na = 3 if bi < 2 else 2
                        bk_ = acc_bank(ui, bi)
                        src_ = ps[bk_][:, 0:390].rearrange("p (a c) -> p a c", a=3)[:, 0:na, 0:129]
                        evac(Ost[:, 3 * bi:3 * bi + na, :], src_, [("ps", bk_)], [("Ost", bi)], eng="dve")
                    combine(og, h, 0)
                    if pending:
                        pending.pop(0)()

            def combine(og, h, aset):
                ab = og % 2
                if DBG.get('cstep', 99) < 1:
                    return
                ctx.op("dve", lambda e: e.reciprocal(out=rz[:, 0:8], in_=Ost[:, :, 128]), reads=[("Ost", 0), ("Ost", 1), ("Ost", 2)], writes=["rz0"])
                if DBG.get('cstep', 99) < 2:
                    return
                pass
                if DBG.get('cstep', 99) < 3:
                    return
                ctx.op("dve", lambda e: e.tensor_scalar(out=rz[:, 4:8], in0=rz[:, 4:8], scalar1=lam_sb[:, 1:2], scalar2=None, op0=ALU.mult),
                       reads=["rz0", "lam"], writes=["rz1b"])
                if DBG.get('cstep', 99) < 4:
                    return
                ctx.op("dve", lambda e: e.tensor_tensor(out=o1, in0=Ost[:, 0:4, 0:128], in1=rz[:, 0:4].unsqueeze(2).to_broadcast([128, 4, 128]), op=ALU.mult),
                       reads=[("Ost", 0), ("Ost", 1), "rz0"], writes=["o1"])
                if DBG.get('cstep', 99) < 5:
                    return
                ctx.op("dve", lambda e: e.tensor_tensor(out=o2, in0=Ost[:, 4:8, 0:128], in1=rz[:, 4:8].unsqueeze(2).to_broadcast([128, 4, 128]), op=ALU.mult),
                       reads=[("Ost", 1), ("Ost", 2), "rz1b"], writes=["o2"])
                if DBG.get('cstep', 99) < 6:
                    return
                ctx.op("dve", lambda e: e.tensor_tensor(out=o1, in0=o1, in1=o2, op=ALU.add), reads=["o1", "o2"], writes=["o1"])
                if DBG.get('cstep', 99) < 7:
                    return
                ctx.op("dve", lambda e: e.tensor_tensor(out=o2, in0=o1, in1=o1, op=ALU.mult), reads=["o1"], writes=["o2"])
                if DBG.get('cstep', 99) < 8:
                    return
                b = rot3.next()
                ss = ssb[b]
                if DBG.get('cstep', 99) < 9:
                    return
                ctx.op("dve", lambda e: e.tensor_reduce(out=ss[:, 0:4], in_=o2, axis=AX.X, op=ALU.add), reads=["o2"], writes=[("ss", b)])
                if DBG.get('cstep', 99) < 10:
                    return
                rs, rkey = rstd_of(ss[:, 0:4], ("ss", b), 128, 4, 1.0 / 128)
                if DBG.get('cstep', 99) < 11:
                    return
                ctx.op("dve", lambda e: e.tensor_tensor(out=o1, in0=o1, in1=rs.unsqueeze(2).to_broadcast([128, 4, 128]), op=ALU.mult),
                       reads=["o1", rkey], writes=["o1"])
                if DBG.get('cstep', 99) < 12:
                    return
                dst_ = attn_all[:, 4 * og:4 * og + 4, (4 * hp + h) * 128:(4 * hp + h + 1) * 128]
                ctx.op("dve", lambda e: e.tensor_tensor(out=dst_, in0=o1,
                                                        in1=subw.unsqueeze(1).to_broadcast([128, 4, 128]), op=ALU.mult),
                       reads=["o1", "subw"], writes=[("attn_all", og, 4 * hp + h)])

            LOOK = 1
            for k in range(min(LOOK, len(items))):
                emit_qk(k)
            if DBG.get('max_items'):
                items = items[:DBG['max_items']]
            for k in range(len(items)):
                if k + LOOK < len(items):
                    emit_qk(k + LOOK)
                emit_rest(k)
            while pending:
                pending.pop(0)()
            if hp == 1 or upto == 3:
                ctx.barrier()
            if hp == 0:
                stop(3)

        if debug:
            ctx.dma("sp", lambda e: e.dma_start(out=dbg["d_attnT"], in_=attn_all.rearrange("p a b -> p (a b)")), "dbg", reads=[])
        stop(4)

        A.off = pass_mark
        p3_mark = A.off
        A.holes = [(off_dead, dead_end)]
        h1b = [A.alloc((1024,), F32) for _ in range(2)]
        wao = A.alloc((8, 1024), BF16)
        wo = A.alloc((8, 1024), BF16)
        uT = A.alloc((4, 144), F32)
        s2 = A.alloc((4, 144), F32)
        s4 = A.alloc((4, 144), F32)
        s8 = A.alloc((4, 144), F32)
        s16 = A.alloc((144,), F32)
        pooledT = A.alloc((4, 128), BF16)
        sg_ = [A.alloc((512,), F32) for _ in range(2)]
        t1 = A.alloc((512,), F32)
        t2 = A.alloc((512,), F32)
        mrg = A.alloc((1024,), BF16)
        mrgT = A.alloc((8, 128), BF16)

        w_in_v = w_in.rearrange("(c p) n -> p c n", p=128)
        w_ao_v = w_ao.rearrange("(c p) n -> p c n", p=128)
        w_o_v = w_o.rearrange("(c p) n -> p c n", p=128)

        rot_m3 = Rot([2, 3, 4, 5, 6, 7])
        xt3 = [xt[0], xt[1]] + [A.alloc((1024,), F32) for _ in range(2)]
        xnTo = [A.alloc((8, 144), BF16) for _ in range(3)]
        poT2 = [A.alloc((4, 128), BF16) for _ in range(2)]
        aT2 = [A.alloc((8, 128), BF16) for _ in range(2)]

        def p3_A2a(i):
            xb = i % 3
            pq = i % 2
            poT, aT = poT2[pq], aT2[pq]
            pbu = rot_m3.next()
            pbu2 = rot_m3.next()
            for g in range(4):
                bank = pbu if g < 2 else pbu2
                c0 = (g % 2) * 256
                for c in range(8):
                    ctx.op("pe", lambda e, g=g, c=c, bank=bank, c0=c0: e.matmul(ps[bank][:, c0:c0 + 144], lhsT=wu_[:, c, g * 128:(g + 1) * 128],
                                                                            rhs=xnTo[xb][:, c, :], start=(c == 0), stop=(c == 7)),
                           reads=[("xnTo", xb), ("xnTo_h", xb), "wu"], writes=[("ps", bank)], inc=(c == 7))
            for bi, bank in enumerate((pbu, pbu2)):
                evac(uT[:, 2 * bi:2 * bi + 2, :], ps[bank][:].rearrange("p (a b) -> p a b", a=2)[:, :, 0:144], [("ps", bank)], [("uT", bi)])
            ukeys = [("uT", 0), ("uT", 1)]
            ctx.op("dve", lambda e: e.tensor_tensor(out=s2[:, :, 1:144], in0=uT[:, :, 1:144], in1=uT[:, :, 0:143], op=ALU.add), reads=ukeys, writes=["s2"])
            ctx.op("dve", lambda e: e.tensor_tensor(out=s4[:, 1:4, 3:144], in0=s2[:, 1:4, 3:144], in1=s2[:, 1:4, 1:142], op=ALU.add), reads=["s2"], writes=["s4"])
            ctx.op("dve", lambda e: e.tensor_tensor(out=s8[:, 2:4, 7:144], in0=s4[:, 2:4, 7:144], in1=s4[:, 2:4, 3:140], op=ALU.add), reads=["s4"], writes=["s8"])
            ctx.op("dve", lambda e: e.tensor_tensor(out=s16[:, 15:144], in0=s8[:, 3, 15:144], in1=s8[:, 3, 7:136], op=ALU.add), reads=["s8"], writes=["s16"])
            for g, (src_, skey, wdt) in enumerate(((s2[:, 0, 16:144], "s2", 2), (s4[:, 1, 16:144], "s4", 4), (s8[:, 2, 16:144], "s8", 8), (s16[:, 16:144], "s16", 16))):
                ctx.op("dve", lambda e, g=g, src_=src_, wdt=wdt: e.scalar_tensor_tensor(out=pooledT[:, g, :], in0=src_, scalar=1.0 / wdt, in1=uT[:, g, 16:144],
                                                                                 op0=ALU.mult, op1=ALU.subtract),
                       reads=[skey] + ukeys, writes=[("pooledT", g)])
            pba = rot_pT.next()
            pTa = ps[pba][:].bitcast(BF16)
            for c in range(8):
                ctx.op("pe", lambda e, c=c: e.transpose(out=pTa[:, c * 128:(c + 1) * 128], in_=attn_all[:, i, c * 128:(c + 1) * 128], identity=idb),
                       reads=["idb"], writes=[("ps", pba)], inc=(c == 7))
            evac(aT, pTa.rearrange("p (c t) -> p c t", c=8), [("ps", pba)], [("aT", pq)], eng="act")

        def p3_A2b(i):
            pq = i % 2
            poT = poT2[pq]
            pbm = rot_m3.next()
            for g in range(4):
                ctx.op("pe", lambda e, g=g: e.matmul(ps[pbm][:, g * 128:(g + 1) * 128], lhsT=gw[:, g, :], rhs=pooledT[:, g, :], start=True, stop=True),
                       reads=[("pooledT", g), "gw"], writes=[("ps", pbm)], inc=(g == 3))
            ctx.op("dve", lambda e: e.tensor_tensor(out=poT, in0=ps[pbm][:].rearrange("p (g t) -> p g t", g=4),
                                                    in1=psc.unsqueeze(2).to_broadcast([128, 4, 128]), op=ALU.mult),
                   reads=[("ps", pbm), "psc"], writes=[("poT", pq)])

        def p3_B1(i):
            xb = i % 3
            pq = i % 2
            poT, aT = poT2[pq], aT2[pq]
            for hf in range(2):
                cs = slice(hf * 512, (hf + 1) * 512)
                b_pp, b_ap, b_gp, b_ga = (rot_m3.next() for _ in range(4))
                for c in range(8):
                    ctx.op("pe", lambda e, c=c, cs=cs, b=b_gp: e.matmul(ps[b][:], lhsT=xnTo[xb][:, c, 16:144], rhs=wgt[:, c, cs], start=(c == 0), stop=(c == 7)),
                           reads=[("xnTo", xb), ("wgt", hf)], writes=[("ps", b_gp)], inc=(c == 7))
                for c in range(8):
                    ctx.op("pe", lambda e, c=c, hf=hf, b=b_ga: e.matmul(ps[b][:], lhsT=xnTo[xb][:, c, 16:144], rhs=wgt[:, c, 1024 + hf * 512:1024 + (hf + 1) * 512],
                                                                  start=(c == 0), stop=(c == 7)),
                           reads=[("xnTo", xb), ("wgt", 2 + hf)], writes=[("ps", b_ga)], inc=(c == 7))
                for c in range(8):
                    ctx.op("pe", lambda e, c=c, cs=cs, b=b_ap: e.matmul(ps[b][:], lhsT=aT[:, c, :], rhs=wao[:, c, cs], start=(c == 0), stop=(c == 7)),
                           reads=[("wao", hf), ("aT", pq)], writes=[("ps", b_ap)], inc=(c == 7))
                for g in range(4):
                    ctx.op("pe", lambda e, g=g, cs=cs, b=b_pp: e.matmul(ps[b][:], lhsT=poT[:, g, :], rhs=wpo[:, g, cs], start=(g == 0), stop=(g == 3)),
                           reads=[("poT", pq), "wpo"], writes=[("ps", b_pp)], inc=(g == 3))
                ctx.op("act", lambda e, b=b_gp: e.activation(out=sg_[0], in_=ps[b][:], func=AF.Sigmoid), reads=[("ps", b_gp)], writes=[("sg", 0)])
                ctx.op("act", lambda e, b=b_ga: e.activation(out=sg_[1], in_=ps[b][:], func=AF.Sigmoid), reads=[("ps", b_ga)], writes=[("sg", 1)])
                ctx.op("dve", lambda e, b=b_pp: e.tensor_tensor(out=t1, in0=sg_[0], in1=ps[b][:], op=ALU.mult), reads=[("sg", 0), ("ps", b_pp)], writes=["t1"])
                ctx.op("dve", lambda e, b=b_ap: e.tensor_tensor(out=t2, in0=sg_[1], in1=ps[b][:], op=ALU.mult), reads=[("sg", 1), ("ps", b_ap)], writes=["t2"])
                ctx.op("dve", lambda e, cs=cs: e.tensor_tensor(out=mrg[:, cs], in0=t1, in1=t2, op=ALU.add), reads=["t1", "t2"], writes=[("mrg", hf)])

        def p3_B2(i):
            xr = xt3[i % 4]
            xrk = ("xt", i % 4)
            hb_ = i % 2
            pb = rot_pT.next()
            pT = ps[pb][:].bitcast(BF16)
            for c in range(8):
                ctx.op("pe", lambda e, c=c: e.transpose(out=pT[:, c * 128:(c + 1) * 128], in_=mrg[:, c * 128:(c + 1) * 128], identity=idb),
                       reads=[("mrg", 0), ("mrg", 1), "idb"], writes=[("ps", pb)], inc=(c == 7))
            evac(mrgT, pT.rearrange("p (c t) -> p c t", c=8), [("ps", pb)], ["mrgT"], eng="act")
            for hf in range(2):
                cs = slice(hf * 512, (hf + 1) * 512)
                b = rot_m3.next()
                for c in range(8):
                    ctx.op("pe", lambda e, c=c, cs=cs, b=b: e.matmul(ps[b][:], lhsT=mrgT[:, c, :], rhs=wo[:, c, cs], start=(c == 0), stop=(c == 7)),
                           reads=["mrgT", ("wo", hf)], writes=[("ps", b)], inc=(c == 7))
                ctx.op("dve", lambda e, cs=cs, b=b: e.tensor_tensor(out=h1b[hb_][:, cs], in0=xr[:, cs], in1=ps[b][:], op=ALU.add),
                       reads=[xrk, ("ps", b)], writes=[("h1b", hb_, hf)])
            ctx.dma("sp", lambda e: e.dma_start(out=h1d[i * 128:(i + 1) * 128, :], in_=h1b[hb_]), "h1w_%d" % hb_,
                    reads=[("h1b", hb_, 0), ("h1b", hb_, 1)], writes=[("h1d", i)])

        l3tok = {}

        def p3_load(i):
            xb3, xb4 = i % 3, i % 4
            xbuf = xt3[xb4]
            ctx.dma("sp", lambda e: e.dma_start(out=xbuf, in_=xo[i, 16:144, :]), "x_%d" % xb4, writes=[("xt", xb4)])
            d_own = xnTo[xb3][:, :, 16:144]
            d_halo = xnTo[xb3][:, :, 0:16]
            for nm, dst_, src_, key_ in (("xlo_%d" % xb3, d_own, xnd[NSLOT + i], ("xnTo", xb3)),
                                         ("xlh_%d" % xb3, d_halo, xnd[2 * i + 1][:, :, 112:128], ("xnTo_h", xb3))):
                q_ = "pool" if nm.startswith("xlh") else "sp"
                l3tok[nm] = ctx.dma(q_, lambda e, dst_=dst_, src_=src_: e.dma_start(out=dst_, in_=src_), nm, writes=[key_],
                                    extra=[l3tok[nm]] if nm in l3tok else [])

        def consume3(i):
            p3_A2a(i)
            if i > 0:
                p3_B2(i - 1)
            p3_A2b(i)
            p3_B1(i)

        p3_load(0)
        for wdst, wsrc_, wnm in ((wao, w_ao_v, "wao"), (wo, w_o_v, "wo")):
            for hf_ in range(2):
                cs_ = slice(hf_ * 512, (hf_ + 1) * 512)
                ctx.dma("pool", lambda e, wdst=wdst, wsrc_=wsrc_, cs_=cs_: e.dma_start(out=wdst[:, :, cs_], in_=wsrc_[:, :, cs_]),
                        "w_%s%d" % (wnm, hf_), writes=[(wnm, hf_)])
        p3_load(1)
        ctx.dma("sp", lambda e: e.dma_start(out=nw, in_=normw[1]), "c_nw", writes=["nw"])
        for i in range(NOWN):
            if i + 2 < NOWN:
                p3_load(i + 2)
            consume3(i)
        p3_B2(NOWN - 1)
        ctx.barrier()
        if debug:
            ctx.dma("sp", lambda e: e.dma_start(out=dbg["d_h1"], in_=h1d), "dbg", reads=[])
        stop(5)

        A.holes = []
        A.off = m0
        wd = A.alloc((NJ, 1024), BF16)
        xr4 = [A.alloc((1024,), F32) for _ in range(2)]
        hnT = A.alloc((8, 1024), BF16)
        actT = A.alloc((NJ, 1024), BF16)
        wgb = [A.alloc((8, 256), BF16) for _ in range(2)]
        wub = [A.alloc((8, 256), BF16) for _ in range(2)]
        sil = [A.alloc((512,), BF16) for _ in range(2)]
        h2 = [A.alloc((1024,), F32) for _ in range(2)]
        yo = [A.alloc((1024,), F32) for _ in range(2)]
        fnw = A.alloc((1024,), F32)
        xt4 = [xt[0], xt[1], A.alloc((1024,), F32), A.alloc((1024,), F32)]

        ctx.dma("sp", lambda e: e.dma_start(out=fnw, in_=normw[2]), "c_fnw", writes=["fnw"])
        w_g_v = w_g.rearrange("(c p) n -> p c n", p=128)
        w_u_v = w_u.rearrange("(c p) n -> p c n", p=128)
        w_d_v = w_d.rearrange("(j p) n -> p j n", p=128)
        rot_m4 = Rot([2, 3, 4, 5, 6, 7])
        rot_w = Rot([0, 1])
        rot_sil = Rot([0, 1])
        rot_y = Rot([0, 1])
        rot_x4 = Rot([0, 1])
        def p4_jobs(th):
            jobs4 = []
            for il in range(8):
                i = th * 8 + il
                xb = il % 4
                xbuf = xt4[xb]

                def ld4(i=i, xb=xb, xbuf=xbuf):
                    ctx.dma("sp", lambda e: e.dma_start(out=xbuf, in_=h1d[i * 128:(i + 1) * 128, :]), "x_%d" % xb,
                            reads=[("h1d", i)], writes=[("xt", xb)])
                jobs4.append(dict(load=ld4, src=xbuf, src_key=("xt", xb), n=128, dst=hnT[:, :, il * 128:(il + 1) * 128], dst_key=("hnT", il)))
            return jobs4

        w_issued = {}

        def p4_issue_w(th, jb):
            wb = rot_w.next()
            cs = slice(jb * 256, (jb + 1) * 256)
            ctx.dma("pool", lambda e: e.dma_start(out=wgb[wb], in_=w_g_v[:, :, cs]), "w_g%d" % wb, writes=[("wgb", wb)])
            ctx.dma("pool", lambda e: e.dma_start(out=wub[wb], in_=w_u_v[:, :, cs]), "w_u%d" % wb, writes=[("wub", wb)])
            w_issued[(th, jb)] = wb
            return wb

        def p4_gateup(th, jbs=range(11), tgs=(0, 1)):
            for jb in jbs:
                if (th, jb) in w_issued:
                    wb = w_issued[(th, jb)]
                else:
                    wb = p4_issue_w(th, jb)
                if th == 0 and jb in (1, 2):
                    q = jb - 1
                    ctx.dma("pool", lambda e, q=q: e.dma_start(out=wd[:, q * 11:(q + 1) * 11, :], in_=w_d_v[:, q * 11:(q + 1) * 11, :]),
                            "w_wd%d" % q, writes=[("wd", q)])
                for jj in range(2):
                    j = 2 * jb + jj
                    for tg in tgs:
                        ts_ = slice(tg * 512, (tg + 1) * 512)
                        bg, bu = rot_m4.next(), rot_m4.next()
                        hkeys = [("hnT", il) for il in range(4 * tg, 4 * tg + 4)]
                        for c in range(8):
                            ctx.op("pe", lambda e, c=c, jj=jj, wb=wb, ts_=ts_, bg=bg: e.matmul(ps[bg][:], lhsT=wgb[wb][:, c, jj * 128:(jj + 1) * 128], rhs=hnT[:, c, ts_],
                                                                                        start=(c == 0), stop=(c == 7)),
                                   reads=hkeys + [("wgb", wb)], writes=[("ps", bg)], inc=(c == 7))
                        for c in range(8):
                            ctx.op("pe", lambda e, c=c, jj=jj, wb=wb, ts_=ts_, bu=bu: e.matmul(ps[bu][:], lhsT=wub[wb][:, c, jj * 128:(jj + 1) * 128], rhs=hnT[:, c, ts_],
                                                                                        start=(c == 0), stop=(c == 7)),
                                   reads=hkeys + [("wub", wb)], writes=[("ps", bu)], inc=(c == 7))
                        sb_ = rot_sil.next()
                        ctx.op("act", lambda e, sb_=sb_, bg=bg: e.activation(out=sil[sb_], in_=ps[bg][:], func=AF.Silu), reads=[("ps", bg)], writes=[("sil", sb_)])
                        ctx.op("dve", lambda e, sb_=sb_, bu=bu, j=j, ts_=ts_: e.tensor_tensor(out=actT[:, j, ts_], in0=sil[sb_], in1=ps[bu][:], op=ALU.mult),
                               reads=[("sil", sb_), ("ps", bu)], writes=[("actT", j, tg)])

        def p4_down(th, il):
            i = th * 8 + il
            yb = rot_y.next()
            xb = il % 2
            xres = xr4[xb]
            ctx.dma("sp", lambda e, i=i, xres=xres: e.dma_start(out=xres, in_=h1d[i * 128:(i + 1) * 128, :]), "xr_%d" % xb,
                    reads=[("h1d", i)], writes=[("xr", xb)])
            for hf in range(2):
                cs = slice(hf * 512, (hf + 1) * 512)
                b = rot_m4.next()
                for j in range(NJ):
                    ctx.op("pe", lambda e, j=j, cs=cs, b=b, il=il: e.matmul(ps[b][:], lhsT=actT[:, j, il * 128:(il + 1) * 128], rhs=wd[:, j, cs],
                                                                      start=(j == 0), stop=(j == NJ - 1)),
                           reads=[("actT", j, il // 4), ("wd", j // 11)], writes=[("ps", b)], inc=(j == NJ - 1))
                ctx.op("dve", lambda e, cs=cs, b=b, xb=xb, yb=yb: e.tensor_tensor(out=h2[yb][:, cs], in0=xr4[xb][:, cs], in1=ps[b][:], op=ALU.add),
                       reads=[("xr", xb), ("ps", b)], writes=[("h2", yb, hf)])
            bq = rot3.next()
            ss = ssb[bq]
            ctx.op("act", lambda e, yb=yb, ss=ss: e.activation(out=junk, in_=h2[yb], func=AF.Square, accum_out=ss[:, 0:1]),
                   reads=[("h2", yb, 0), ("h2", yb, 1)], writes=["junk", ("ss", bq)])
            rs, rkey = rstd_of(ss[:, 0:1], ("ss", bq), 128, 1, 1.0 / 1024)
            ctx.op("dve", lambda e, yb=yb, rs=rs: e.scalar_tensor_tensor(out=yo[yb], in0=h2[yb], scalar=rs, in1=fnw, op0=ALU.mult, op1=ALU.mult),
                   reads=[("h2", yb, 0), ("h2", yb, 1), rkey, "fnw"], writes=[("yo", yb)])
            out_stores.append(ctx.dma("sp", lambda e, yb=yb, i=i: e.dma_start(out=out[i * 128:(i + 1) * 128, :], in_=yo[yb]), "o_%d" % yb,
                                      reads=[("yo", yb)]))

        p4_issue_w(0, 0)

        def consume4a(k):
            if k == 3:
                p4_gateup(0, jbs=[0], tgs=[0])
            elif k == 7:
                p4_gateup(0, jbs=[0], tgs=[1])
        norm_pipe(p4_jobs(0), consume4a)
        p4_gateup(0, jbs=range(1, 11))
        p4_issue_w(1, 0)
        p4_issue_w(1, 1)
        norm_pipe(p4_jobs(1), lambda k: p4_down(0, k))
        p4_gateup(1)
        for il in range(8):
            p4_down(1, il)
        ctx.wait_only("sp", out_stores)
        ctx.barrier()
        ctx.dead = False
        print('ninst', {k: len(v) for k, v in ctx.ops.items()}, 'nsem', len(ctx.sem), 'arena peak', A.peak, flush=True)
        ctx.emit()
    return nc


def _t5_bucket(rel):
    nb = 16
    ret = np.where(rel > 0, nb, 0)
    n = np.abs(rel)
    max_exact = 8
    nf = np.maximum(n, max_exact).astype(np.float32)
    large = max_exact + (np.log(nf / np.float32(max_exact)) / np.float32(math.log(128 / max_exact)) * np.float32(nb - max_exact)).astype(np.int32)
    large = np.minimum(large, nb - 1)
    return ret + np.where(n < max_exact, n, large)


def _prep_inputs(inp):
    f = lambda a: np.ascontiguousarray(np.asarray(a, dtype=np.float32))
    x = f(inp["x"])
    meta = f(inp["meta_tokens"])
    table = f(inp["rel_bias_table"])
    B = x.shape[0]
    kl = np.arange(128)[:, None]
    ql = np.arange(128)[None, :]
    bk = np.stack([_t5_bucket(kl - ql), _t5_bucket(kl - ql - 128)], 0)
    biasT = table[bk]
    biasT = np.ascontiguousarray(biasT.transpose(1, 0, 3, 2)).reshape(128, 2 * 8 * 128)
    maskadd = np.where((kl >= 64) & (ql < 64), -30000.0, 0.0).astype(np.float32)
    shared = {
        "w_in": f(inp["w_in"][0]), "pgw": f(inp["pool_group_w"][0]), "w_po": f(inp["w_pool_out"][0]),
        "w_ao": f(inp["w_attn_out"][0]), "w_o": f(inp["w_o"][0]), "w_g": f(inp["w_gate"][0]),
        "w_u": f(inp["w_up"][0]), "w_d": f(inp["w_down"][0]),
        "normw": np.ascontiguousarray(np.broadcast_to(np.stack([f(inp["mix_norm_w"][0]), f(inp["ffn_norm_w"][0]), f(inp["final_norm_w"])], 0)[:, None, :], (3, 128, 1024))),
        "sublnw": np.ascontiguousarray(np.broadcast_to(f(inp["subln_w"][0])[None, :], (128, 128))),
        "lamv": np.ascontiguousarray(np.broadcast_to(np.concatenate([f(inp["lambda_q1"][0]), f(inp["lambda_k1"][0]), f(inp["lambda_q2"][0]), f(inp["lambda_k2"][0])])[None, :], (128, 256))),
        "cfar": np.ascontiguousarray(np.broadcast_to(table[15][None, :], (128, 8))),
        "psc": np.ascontiguousarray(f(inp["pool_scale"][0]).reshape(4, 128).T),
        "ident": np.eye(128, dtype=np.float32),
        "maskadd": maskadd,
        "biasT": biasT,
    }
    in_maps = []
    for c in range(8):
        b, p = c // 2, c % 2
        hf = np.concatenate([meta, x[b]], 0)
        idx = 128 * np.arange(NSLOT)[:, None] + np.arange(128)[None, :] - 112 - 128 * (1 - p)
        valid = (idx >= 0) & (idx < hf.shape[0])
        xk = np.where(valid.reshape(-1)[:, None], hf[np.clip(idx, 0, hf.shape[0] - 1).reshape(-1)], np.float32(0))
        xo = np.stack([hf[128 * (2 * i + p):128 * (2 * i + p) + 144] for i in range(NOWN)], 0)
        m = dict(shared)
        m["xk"] = np.ascontiguousarray(xk, dtype=np.float32)
        m["xo"] = np.ascontiguousarray(xo, dtype=np.float32)
        m["kvalid"] = np.ascontiguousarray(valid.T.astype(np.float32))
        in_maps.append(m)
    return in_maps, B


_NC_CACHE = {}


def kernel(**inputs):
    in_maps, B = _prep_inputs(inputs)
    if "nc" not in _NC_CACHE:
        _NC_CACHE["nc"] = build(0)
    res = run_bass_kernel_spmd(_NC_CACHE["nc"], in_maps, core_ids=list(range(8)))
    out = np.empty((B, 4096, 1024), np.float32)
    for c in range(8):
        b, p = c // 2, c % 2
        y = np.asarray(res.results[c]["out"], dtype=np.float32).reshape(NOWN, 128, 1024)
        for i in range(NOWN):
            g = 2 * i + p
            out[b, 128 * g:128 * (g + 1)] = y[i]
    return out
```

```python
import math
import numpy as np
from contextlib import ExitStack
import concourse.bass as bass
import concourse.mybir as mybir
from concourse.bass_utils import run_bass_kernel_spmd

F32 = mybir.dt.float32
BF16 = mybir.dt.bfloat16
ALU = mybir.AluOpType
AF = mybir.ActivationFunctionType
AX = mybir.AxisListType

NSLOT = 33
NOWN = 16
EPS = 1e-6
DFF = 2816
NJ = DFF // 128
DBG = {}


class Ctx:
    ENG = ("pe", "act", "dve", "pool", "sp")

    def __init__(self, nc, es):
        self.nc, self.es = nc, es
        self.ops = {e: [] for e in self.ENG}
        self.sem = {}
        self.val = {}
        self.waited = {}
        self.lastw = {}
        self.readers = {}
        self.maxwait = {}
        self.dead = False

    def _sem(self, key):
        if key not in self.sem:
            self.sem[key] = self.es.enter_context(self.nc.semaphore("s%d" % len(self.sem)))
            self.val[key] = 0
        return self.sem[key]

    def _deps(self, eng, reads, writes, extra=()):
        toks = list(extra)
        for k in reads:
            if k in self.lastw:
                toks.append(self.lastw[k])
        for k in writes:
            if k in self.lastw:
                toks.append(self.lastw[k])
            toks.extend(self.readers.get(k, {}).items())
        need = {}
        for (sk, v) in toks:
            if sk == "E_pe" and eng == "pe":
                continue
            if self.waited.get((eng, sk), 0) >= v:
                continue
            need[sk] = max(need.get(sk, 0), v)
        for sk, v in need.items():
            self.waited[(eng, sk)] = v
            self.maxwait[sk] = max(self.maxwait.get(sk, 0), v)
        return list(need.items())

    def _commit(self, tok, reads, writes):
        for k in reads:
            d = self.readers.setdefault(k, {})
            d[tok[0]] = max(d.get(tok[0], 0), tok[1])
        for k in writes:
            self.lastw[k] = tok
            self.readers[k] = {}

    def op(self, eng, fn, reads=(), writes=(), inc=True, extra=()):
        if self.dead:
            return None
        waits = self._deps(eng, reads, writes, extra)
        sk = "E_" + eng
        self._sem(sk)
        tok = (sk, self.val[sk] + 1)
        if inc:
            self.val[sk] += 1
        self.ops[eng].append((fn, waits, (sk, 1) if inc else None))
        self._commit(tok, reads, writes)
        return tok

    def dma(self, eng, fn, semkey, reads=(), writes=(), extra=()):
        if self.dead:
            return None
        waits = self._deps(eng, reads, writes, extra)
        self._sem(semkey)
        self.val[semkey] += 16
        tok = (semkey, self.val[semkey])
        self.ops[eng].append((fn, waits, (semkey, 16)))
        self._commit(tok, reads, writes)
        return tok

    def wait_only(self, eng, toks):
        if self.dead:
            return
        waits = self._deps(eng, (), (), toks)
        self.ops[eng].append((None, waits, None))

    def barrier(self):
        if self.dead:
            return
        toks = [(sk, v) for sk, v in self.val.items() if v > 0]
        for e in self.ENG:
            self.wait_only(e, toks)
        self.lastw = {}
        self.readers = {}

    def emit(self):
        for sk, v in self.maxwait.items():
            assert v <= self.val[sk], ("wait beyond final count", sk, v, self.val[sk])
        nc = self.nc
        with nc.Block() as block:
            def mk(name):
                def run(e):
                    for fn, waits, inc in self.ops[name]:
                        for sk, v in waits:
                            e.wait_ge(self.sem[sk], v)
                        if fn is None:
                            continue
                        ins = fn(e)
                        if inc is not None:
                            ins.then_inc(self.sem[inc[0]], inc[1])
                return run
            block.tensor(mk("pe"))
            block.scalar(mk("act"))
            block.vector(mk("dve"))
            block.gpsimd(mk("pool"))
            block.sync(mk("sp"))


class Arena:
    def __init__(self, t, nbytes):
        self.t, self.nbytes, self.off = t, nbytes, 0
        self.holes = []

    def alloc(self, shape, dt):
        n = int(np.prod(shape))
        sz = 4 if dt == F32 else 2
        self.off = (self.off + 3) // 4 * 4
        for lo, hi in self.holes:
            if self.off < hi and self.off + n * sz > lo:
                self.off = hi
        a, b = self.off // 2, (self.off + n * sz) // 2
        self.off += n * sz
        assert self.off <= self.nbytes, ("arena overflow", self.off, self.nbytes)
        self.peak = max(getattr(self, 'peak', 0), self.off)
        ap = self.t[:, a:b]
        if dt == F32:
            ap = ap.bitcast(F32)
        if len(shape) == 2:
            ap = ap.rearrange("p (a b) -> p a b", a=shape[0])
        elif len(shape) == 3:
            ap = ap.rearrange("p (a b c) -> p a b c", a=shape[0], b=shape[1])
        return ap


class Rot:
    def __init__(self, items):
        self.items, self.i = items, 0

    def next(self):
        it = self.items[self.i % len(self.items)]
        self.i += 1
        return it


def build(debug=0, upto=0):
    nc = bass.Bass("TRN2", target_bir_lowering=False)

    def din(name, shape, dt=F32):
        return nc.dram_tensor(name, shape, dt, kind="ExternalInput").ap()

    xk = din("xk", [NSLOT * 128, 1024])
    xo = din("xo", [NOWN, 144, 1024])
    w_in = din("w_in", [1024, 5632])
    pgw = din("pgw", [4, 128, 128])
    w_po = din("w_po", [512, 1024])
    w_ao = din("w_ao", [1024, 1024])
    w_o = din("w_o", [1024, 1024])
    w_g = din("w_g", [1024, DFF])
    w_u = din("w_u", [1024, DFF])
    w_d = din("w_d", [DFF, 1024])
    normw = din("normw", [3, 128, 1024])
    sublnw = din("sublnw", [128, 128])
    lamv = din("lamv", [128, 256])
    cfar_d = din("cfar", [128, 8])
    kvalid_d = din("kvalid", [128, NSLOT])
    psc_d = din("psc", [128, 4])
    ident_d = din("ident", [128, 128])
    maskadd_d = din("maskadd", [128, 128])
    biasT_d = din("biasT", [128, 2 * 8 * 128])
    out = nc.dram_tensor("out", [NOWN * 128, 1024], F32, kind="ExternalOutput").ap()
    xnd = nc.dram_tensor("xnd", [NSLOT + NOWN, 128, 8, 128], BF16).ap()
    h1d = nc.dram_tensor("h1d", [NOWN * 128, 1024], F32).ap()
    dbg = {}
    if debug:
        for nm, shp, dt in (("d_kt", [128, 4 * NSLOT * 128], BF16), ("d_vx", [128, NSLOT * 4 * 129], BF16),
                            ("d_qt", [128, 4 * 2048], BF16), ("d_attnT", [128, 8 * 2048], BF16),
                            ("d_h1", [NOWN * 128, 1024], F32)):
            dbg[nm] = nc.dram_tensor(nm, shp, dt, kind="ExternalOutput").ap()

    ARENA_BYTES = 212000
    with ExitStack() as es:
        arena_t = es.enter_context(nc.sbuf_tensor("arena", [128, ARENA_BYTES // 2], BF16))
        pall_t = es.enter_context(nc.psum_tensor("pall", [128, 4096], F32))
        pall = pall_t[:]
        ps = [pall[:, i * 512:(i + 1) * 512] for i in range(8)]
        ctx = Ctx(nc, es)
        A = Arena(arena_t, ARENA_BYTES)

        def stop(n):
            if upto == n and not ctx.dead:
                ctx.barrier()
                ctx.dead = True

        idf = A.alloc((128,), F32)
        idb = A.alloc((128,), BF16)
        nhalf = A.alloc((4,), F32)
        lam_sb = A.alloc((2,), F32)
        cfar = A.alloc((8,), F32)
        ncfar = A.alloc((8,), F32)
        kvalid = A.alloc((NSLOT,), F32)
        psc = A.alloc((4,), F32)
        subw = A.alloc((128,), F32)
        nw = A.alloc((1024,), F32)
        E = A.alloc((2, 8, 128), BF16)
        ssb = [A.alloc((4,), F32) for _ in range(3)]
        varb = [A.alloc((4,), F32) for _ in range(3)]
        rstdb = [A.alloc((4,), F32) for _ in range(3)]
        pss = [A.alloc((2,), F32) for _ in range(4)]
        pvar = [A.alloc((2,), F32) for _ in range(4)]
        prs = [A.alloc((2,), F32) for _ in range(4)]
        junk = A.alloc((1024,), BF16)
        xh = [A.alloc((1024,), BF16) for _ in range(2)]
        xt = [A.alloc((1024,), F32) for _ in range(2)]
        base_mark = A.off

        def cdma(dst, src, key):
            ctx.dma("sp", lambda e: e.dma_start(out=dst, in_=src), "c_" + key, writes=[key])

        cdma(idf, ident_d, "idf")
        cdma(cfar, cfar_d, "cfar")
        cdma(kvalid, kvalid_d, "kvalid")
        cdma(psc, psc_d, "psc")
        cdma(subw, sublnw, "subw0")
        cdma(nw, normw[0], "nw")
        ctx.op("dve", lambda e: e.tensor_copy(out=idb, in_=idf), reads=["idf"], writes=["idb"])
        ctx.op("pool", lambda e: e.memset(nhalf, -0.5), writes=["nhalf"])
        ctx.op("dve", lambda e: e.tensor_scalar(out=ncfar, in0=cfar, scalar1=-1.0, scalar2=None, op0=ALU.mult),
               reads=["cfar"], writes=["ncfar"])
        ctx.op("dve", lambda e: e.tensor_scalar(out=subw, in0=subw, scalar1=0.8, scalar2=None, op0=ALU.mult),
               reads=["subw0"], writes=["subw"])
        m0 = A.off
        lv = A.alloc((256,), F32)
        lp = A.alloc((128,), F32)
        lsum = A.alloc((2,), F32)
        lexp = A.alloc((2,), F32)
        bT = A.alloc((2, 8, 128), F32)
        madd = A.alloc((128,), F32)
        cdma(lv, lamv, "lv")
        cdma(bT.rearrange("p a b c -> p (a b c)"), biasT_d, "bT")
        cdma(madd, maskadd_d, "madd")
        def setup_lam_E():
            ctx.op("dve", lambda e: e.tensor_tensor(out=lp.rearrange("p (a b) -> p a b", a=2),
                                                    in0=lv.rearrange("p (a b c) -> p a b c", a=2, b=2)[:, :, 0, :],
                                                    in1=lv.rearrange("p (a b c) -> p a b c", a=2, b=2)[:, :, 1, :], op=ALU.mult),
                   reads=["lv"], writes=["lp"])
            ctx.op("dve", lambda e: e.tensor_reduce(out=lsum, in_=lp.rearrange("p (a b) -> p a b", a=2), axis=AX.X, op=ALU.add),
                   reads=["lp"], writes=["lsum"])
            ctx.op("act", lambda e: e.activation(out=lexp, in_=lsum, func=AF.Exp), reads=["lsum"], writes=["lexp"])
            ctx.op("dve", lambda e: e.tensor_tensor(out=lam_sb[:, 0:1], in0=lexp[:, 0:1], in1=lexp[:, 1:2], op=ALU.subtract),
                   reads=["lexp"], writes=["lam0"])
            ctx.op("dve", lambda e: e.tensor_scalar(out=lam_sb[:, 0:1], in0=lam_sb[:, 0:1], scalar1=0.2, scalar2=None, op0=ALU.add),
                   reads=["lam0"], writes=["lam1"])
            ctx.op("dve", lambda e: e.tensor_scalar(out=lam_sb[:, 1:2], in0=lam_sb[:, 0:1], scalar1=-1.0, scalar2=None, op0=ALU.mult),
                   reads=["lam1"], writes=["lam"])
            ctx.op("dve", lambda e: e.tensor_tensor(out=bT[:, 0, :, :], in0=bT[:, 0, :, :],
                                                    in1=madd.unsqueeze(1).to_broadcast([128, 8, 128]), op=ALU.add),
                   reads=["bT", "madd"], writes=["bT2"])
            for ty in range(2):
                for hh in range(8):
                    ctx.op("act", lambda e, ty=ty, hh=hh: e.activation(out=E[:, ty, hh, :], in_=bT[:, ty, hh, :], func=AF.Exp,
                                                                       bias=ncfar[:, hh:hh + 1], scale=1.0),
                           reads=["bT2", "ncfar"], writes=[("E", ty, hh)])

        A.off = m0
        stop(1)

        rot3 = Rot([0, 1, 2])
        rot_xh = Rot([0, 1])
        rot_pT = Rot([0, 1])
        evac_flip = [0]

        def evac(out_ap, in_ap, reads, writes, eng=None):
            evac_flip[0] ^= 1
            if (evac_flip[0] and eng is None) or eng == "act":
                return ctx.op("act", lambda e: e.activation(out=out_ap, in_=in_ap, func=AF.Copy), reads=reads, writes=writes)
            return ctx.op("dve", lambda e: e.tensor_copy(out=out_ap, in_=in_ap), reads=reads, writes=writes)

        def rstd_of(ss_ap, ss_key, n, k, inv_n):
            b = rot3.next()
            var, rs = varb[b], rstdb[b]
            ctx.op("pool", lambda e: e.tensor_scalar(out=var[:n, :k], in0=ss_ap, scalar1=inv_n, scalar2=EPS, op0=ALU.mult, op1=ALU.add),
                   reads=[ss_key], writes=[("var", b)])
            ctx.op("pool", lambda e: e.tensor_tensor(out=rs[:n, :k], in0=var[:n, :k], in1=nhalf[:n, :k], op=ALU.pow),
                   reads=[("var", b), "nhalf"], writes=[("rstd", b)])
            return rs[:n, :k], ("rstd", b)

        def norm_T(src, src_key, n, dst, dst_key):
            b = rot3.next()
            ss = ssb[b]
            ctx.op("act", lambda e: e.activation(out=junk[:n, :], in_=src, func=AF.Square, accum_out=ss[:n, 0:1]),
                   reads=[src_key], writes=["junk", ("ss", b)])
            rs, rkey = rstd_of(ss[:n, 0:1], ("ss", b), n, 1, 1.0 / 1024)
            hb = rot_xh.next()
            ctx.op("dve", lambda e: e.scalar_tensor_tensor(out=xh[hb][:n, :], in0=src, scalar=rs, in1=nw[:n, :], op0=ALU.mult, op1=ALU.mult),
                   reads=[src_key, rkey, "nw"], writes=[("xh", hb)])
            pb = rot_pT.next()
            pT = ps[pb][:].bitcast(BF16)
            for c in range(8):
                ctx.op("pe", lambda e, c=c: e.transpose(out=pT[:, c * 128:c * 128 + n], in_=xh[hb][:n, c * 128:(c + 1) * 128], identity=idb[:n, :n]),
                       reads=[("xh", hb), "idb"], writes=[("ps", pb)], inc=(c == 7))
            evac(dst, pT.rearrange("p (c t) -> p c t", c=8)[:, :, :n], [("ps", pb)], [dst_key])

        def norm_pipe(jobs, consume, s2off=2, xhb=None, evac_eng=None):
            xhb = xhb or xh
            nxh = len(xhb)
            N = len(jobs)

            def S0(k):
                if jobs[k].get("load") is not None:
                    jobs[k]["load"]()

            def S1(k):
                j = jobs[k]
                n, b = j["n"], k % 4
                src_, ss = j["src"], pss[b]
                ctx.op("act", lambda e: e.activation(out=junk[:n, :], in_=src_, func=AF.Square, accum_out=ss[:n, 0:1]),
                       reads=[j["src_key"]], writes=["junk", ("pss", b)])
                ctx.op("pool", lambda e: e.tensor_scalar(out=pvar[b][:n, 0:1], in0=ss[:n, 0:1], scalar1=1.0 / 1024, scalar2=EPS, op0=ALU.mult, op1=ALU.add),
                       reads=[("pss", b)], writes=[("pvar", b)])
                ctx.op("pool", lambda e: e.tensor_tensor(out=prs[b][:n, 0:1], in0=pvar[b][:n, 0:1], in1=nhalf[:n, 0:1], op=ALU.pow),
                       reads=[("pvar", b), "nhalf"], writes=[("prs", b)])

            def S2(k):
                j = jobs[k]
                n, b, hb = j["n"], k % 4, k % nxh
                src_ = j["src"]
                ctx.op("dve", lambda e: e.scalar_tensor_tensor(out=xhb[hb][:n, :], in0=src_, scalar=prs[b][:n, 0:1], in1=nw[:n, :], op0=ALU.mult, op1=ALU.mult),
                       reads=[j["src_key"], ("prs", b), "nw"], writes=[("xh", hb)])

            def S3(k):
                j = jobs[k]
                n, hb = j["n"], k % nxh
                pb = rot_pT.next()
                pT = ps[pb][:].bitcast(BF16)
                for c in range(8):
                    ctx.op("pe", lambda e, c=c: e.transpose(out=pT[:, c * 128:c * 128 + n], in_=xhb[hb][:n, c * 128:(c + 1) * 128], identity=idb[:n, :n]),
                           reads=[("xh", hb), "idb"], writes=[("ps", pb)], inc=(c == 7))
                evac(j["dst"], pT.rearrange("p (c t) -> p c t", c=8)[:, :, :n], [("ps", pb)], [j["dst_key"]], eng=evac_eng)

            for it in range(-(s2off + 2), N):
                for stage, off in ((S0, s2off + 2), (S1, s2off + 1), (S2, s2off), (S3, 1)):
                    if 0 <= it + off < N:
                        stage(it + off)
                if it >= 0:
                    consume(it)

        attn_all = A.alloc((NOWN, 1024), BF16)
        pass_mark = A.off
        rot_mm = Rot([2, 3, 4, 5])
        out_stores = []

        for hp in range(2):
            A.off = pass_mark
            KT = A.alloc((4, NSLOT * 128), BF16)
            Vx = A.alloc((NSLOT, 4, 129), BF16)
            QT = A.alloc((4, 2048), BF16)
            off_dead = (A.off + 3) // 4 * 4
            wq = A.alloc((8, 512), BF16)
            wk = A.alloc((8, 512), BF16)
            wv = A.alloc((8, 512), BF16)
            xnTg = [A.alloc((8, 512), BF16) for _ in range(3)]
            xtp = [xt[0], xt[1], A.alloc((1024,), F32)]
            PT = [A.alloc((2, 512), BF16) for _ in range(3)]
            Ost = A.alloc((8, 129), F32)
            o1 = A.alloc((4, 128), F32)
            o2 = A.alloc((4, 128), F32)
            rz = A.alloc((8,), F32)

            w_in_v = w_in.rearrange("(c p) n -> p c n", p=128)
            def load_qkv_w(hpx, defer=None):
                for nm, dst, c0 in (("wv", wv, 2560), ("wk", wk, 1536), ("wq", wq, 512)):
                    cs = slice(c0 + hpx * 512, c0 + (hpx + 1) * 512)

                    def issue(nm=nm, dst=dst, cs=cs):
                        ctx.dma("pool", lambda e: e.dma_start(out=dst, in_=w_in_v[:, :, cs]), "w_" + nm, writes=[nm])
                    if defer is None:
                        issue()
                    else:
                        defer.append(issue)
            if hp == 0:
                load_qkv_w(0)
            for h in range(4):
                ctx.op("dve", lambda e, h=h: e.tensor_copy(out=Vx[:, :, h, 128], in_=kvalid), reads=["kvalid"], writes=[("Vx1", h)])

            seq = []
            for sg in range(9):
                tl = list(range(4 * sg, min(4 * sg + 4, NSLOT)))
                for j, t in enumerate(tl):
                    seq.append(("k", sg, j, t, len(tl)))
            for og in range(4):
                for j in range(4):
                    seq.append(("q", 9 + og, j, 4 * og + j, 4))

            def compute(n):
                kind, g, j, t, nt = seq[n]
                gb = g % 3
                if kind == "k":
                    pb = rot_mm.next()
                    for c in range(8):
                        ctx.op("pe", lambda e, c=c: e.matmul(ps[pb][:], lhsT=xnTg[gb][:, c, j * 128:(j + 1) * 128], rhs=wv[:, c, :],
                                                            start=(c == 0), stop=(c == 7)),
                               reads=[("xnTg", gb, j), "wv"], writes=[("ps", pb)], inc=(c == 7))
                    evac(Vx[:, t, :, 0:128], ps[pb][:].rearrange("p (h d) -> p h d", h=4), [("ps", pb)], [("Vx", t)])
                if j != nt - 1:
                    return
                ntok = 128 * nt
                for h in range(4):
                    pbk = rot_mm.next()
                    wsrc, wkey = (wk, "wk") if kind == "k" else (wq, "wq")
                    for c in range(8):
                        ctx.op("pe", lambda e, c=c, h=h, pbk=pbk, wsrc=wsrc: e.matmul(ps[pbk][:, 0:ntok], lhsT=wsrc[:, c, h * 128:(h + 1) * 128], rhs=xnTg[gb][:, c, 0:ntok],
                                                                                 start=(c == 0), stop=(c == 7)),
                               reads=[("xnTg", gb, jj) for jj in range(nt)] + [wkey], writes=[("ps", pbk)], inc=(c == 7))
                    if kind == "k":
                        evac(KT[:, h, g * 512:g * 512 + ntok], ps[pbk][:, 0:ntok], [("ps", pbk)], [("KT", h, g)])
                    else:
                        og_ = g - 9
                        evac(QT[:, h, og_ * 512:(og_ + 1) * 512], ps[pbk][:], [("ps", pbk)], [("QT", h, og_)])

            jobs = []
            for n, (kind, g, j, t, nt) in enumerate(seq):
                xb = n % 3
                src_ap = xk[t * 128:(t + 1) * 128, :] if kind == "k" else xo[t, 16:144, :]
                xdst = xtp[xb]

                def ld(xdst=xdst, src_ap=src_ap, xb=xb):
                    ctx.dma("sp", lambda e: e.dma_start(out=xdst, in_=src_ap), "x_%d" % xb, writes=[("xt", xb)])
                jobs.append(dict(load=ld, src=xdst, src_key=("xt", xb), n=128,
                                 dst=xnTg[g % 3][:, :, j * 128:(j + 1) * 128], dst_key=("xnTg", g % 3, j)))
            xtok = {}
            if hp == 0:
                def compute_and_save(n):
                    kind, g, j, t, nt = seq[n]
                    gb = g % 3
                    tsrc = xnTg[gb][:, :, j * 128:(j + 1) * 128]
                    sk_ = "xs_%d" % (n % 4)
                    xtok[sk_] = ctx.dma("sp", lambda e: e.dma_start(out=xnd[n], in_=tsrc), sk_, reads=[("xnTg", gb, j)], writes=[("xnd", n)],
                                        extra=[xtok[sk_]] if sk_ in xtok else [])
                    compute(n)
                norm_pipe(jobs, compute_and_save)
            else:
                def load_xn(n):
                    kind, g, j, t, nt = seq[n]
                    gb = g % 3
                    tdst = xnTg[gb][:, :, j * 128:(j + 1) * 128]
                    sk_ = "xl_%d" % (n % 6)
                    xtok[sk_] = ctx.dma("sp", lambda e: e.dma_start(out=tdst, in_=xnd[n]), sk_, reads=[("xnd", n)], writes=[("xnTg", gb, j)],
                                        extra=[xtok[sk_]] if sk_ in xtok else [])
                PFX = 5
                for n in range(min(PFX, len(seq))):
                    load_xn(n)
                for n in range(len(seq)):
                    if n + PFX < len(seq):
                        load_xn(n + PFX)
                    compute(n)
            if hp == 0:
                setup_lam_E()
            pending = []
            if hp == 0:
                load_qkv_w(1, pending)
            if hp == 1:
                save_off = A.off
                A.off = off_dead
                wu_ = A.alloc((8, 512), BF16)
                wgt = A.alloc((8, 2048), BF16)
                gw = A.alloc((4, 128), BF16)
                wpo = A.alloc((4, 1024), BF16)
                dead_end = A.off
                assert dead_end <= off_dead + 3 * 8192 + 3 * 8192 + 4096
                A.off = save_off
                deadk = ["wq", "wk", "wv", ("xt", 2)] + [("xnTg", g_, j_) for g_ in range(3) for j_ in range(4)]
                pending.append(lambda: ctx.dma("pool", lambda e: e.dma_start(out=wu_, in_=w_in_v[:, :, 0:512]), "w_wu", writes=["wu"] + deadk))
                pending.append(lambda: ctx.dma("pool", lambda e: e.dma_start(out=gw, in_=pgw.rearrange("g c d -> c g d")), "w_gw", writes=["gw"] + deadk))
                for q in range(4):
                    pending.append(lambda q=q: ctx.dma("pool", lambda e: e.dma_start(out=wgt[:, :, q * 512:(q + 1) * 512], in_=w_in_v[:, :, 3584 + q * 512:3584 + (q + 1) * 512]),
                                                      "w_wgt%d" % q, writes=[("wgt", q)] + deadk))
                pending.append(lambda: ctx.dma("pool", lambda e: e.dma_start(out=wpo, in_=w_po.rearrange("(g p) n -> p g n", p=128)), "w_wpo", writes=["wpo"] + deadk))

            if debug and hp == 0:
                ctx.dma("sp", lambda e: e.dma_start(out=dbg["d_kt"], in_=KT.rearrange("p a b -> p (a b)")), "dbg",
                        reads=[("KT", h, sg) for h in range(4) for sg in range(9)])
                ctx.dma("sp", lambda e: e.dma_start(out=dbg["d_vx"], in_=Vx.rearrange("p a b c -> p (a b c)")), "dbg",
                        reads=[("Vx", t) for t in range(NSLOT)] + [("Vx1", h) for h in range(4)])
                ctx.dma("sp", lambda e: e.dma_start(out=dbg["d_qt"], in_=QT.rearrange("p a b -> p (a b)")), "dbg",
                        reads=[("QT", h, og) for h in range(4) for og in range(4)])

            if hp == 0:
                stop(2)
            rot_s = Rot([0, 2])
            rot_pt = Rot([0, 1, 2])
            units = [(og, h) for og in range(4) for h in range(4)]
            items = []
            for ui, (og, h) in enumerate(units):
                i0 = 4 * og
                for s in range(2 * i0 + 9):
                    items.append((ui, s))

            def acc_bank(ui, bi):
                return 4 + ((-ui) % 4 + bi) % 4

            def acc_ap(ui, a):
                bank = acc_bank(ui, a // 3)
                c0 = (a % 3) * 130
                return ps[bank][:, c0:c0 + 129], ("ps", bank)

            sbank = {}

            def emit_qk(k):
                ui, s = items[k]
                og, h = units[ui]
                i0 = 4 * og
                jmin = max(0, (s - 2 * i0 - 2 + 1) // 2)
                ncol = (4 - jmin) * 128
                sb_ = rot_s.next()
                sbank[k] = sb_
                q0 = og * 512 + jmin * 128
                for m in range(2):
                    ctx.op("pe", lambda e, m=m: e.matmul(ps[sb_ + m][:, 0:ncol], lhsT=KT[m * 64:(m + 1) * 64, h, s * 128:(s + 1) * 128],
                                                         rhs=QT[m * 64:(m + 1) * 64, h, q0:q0 + ncol], start=True, stop=True),
                           reads=[("KT", h, s // 4), ("QT", h, og)], writes=[("ps", sb_ + m)], inc=(m == 1))

            def emit_rest(k, mid=None):
                ui, s = items[k]
                og, h = units[ui]
                hh = 4 * hp + h
                i0 = 4 * og
                nsl = 2 * i0 + 9
                jmin = max(0, (s - 2 * i0 - 2 + 1) // 2)
                ncol = (4 - jmin) * 128
                sb_ = sbank.pop(k)
                pb = rot_pt.next()
                s_in = pall[:, sb_ * 512:(sb_ + 2) * 512].rearrange("p (m c) -> p m c", m=2)[:, :, 0:ncol]
                ctx.op("act", lambda e: e.activation(out=PT[pb][:, :, 0:ncol], in_=s_in, func=AF.Exp,
                                                     bias=cfar[:, hh:hh + 1], scale=0.125),
                       reads=[("ps", sb_), ("ps", sb_ + 1), "cfar"], writes=[("PT", pb, jj) for jj in range(4)])
                d = s - (2 * i0 + 2)
                if d >= -1:
                    if d % 2 == 0:
                        jf, ty = d // 2, 0
                    else:
                        jf, ty = (d + 1) // 2, 1
                    cf = (jf - jmin) * 128
                    ctx.op("dve", lambda e: e.tensor_tensor(out=PT[pb][:, :, cf:cf + 128], in0=PT[pb][:, :, cf:cf + 128],
                                                             in1=E[:, ty, hh, :].unsqueeze(1).to_broadcast([128, 2, 128]), op=ALU.mult),
                           reads=[("PT", pb, jf - jmin), ("E", ty, hh)], writes=[("PT", pb, jf - jmin)])
                if mid is not None:
                    mid()
                jfix = jf if d >= -1 else -1
                pv = [(m, j) for m in range(2) for j in range(jmin, 4) if j != jfix] + [(m, j) for m in range(2) for j in range(jmin, 4) if j == jfix]
                for idx, (m, j) in enumerate(pv):
                    a = m * 4 + j
                    acc, akey = acc_ap(ui, a)
                    first = (s == 0 and a % 3 == 0)
                    last = (s == 2 * (i0 + j) + 2)
                    cj = (j - jmin) * 128
                    ctx.op("pe", lambda e, acc=acc, cj=cj, m=m, first=first, last=last: e.matmul(acc, lhsT=PT[pb][:, m, cj:cj + 128], rhs=Vx[:, s, h, :],
                                                                                           start=first, stop=last, skip_group_check=True),
                           reads=[("PT", pb, j - jmin), ("Vx", s), ("Vx1", h)], writes=[akey], inc=(idx == len(pv) - 1))
                if s == nsl - 1:
                    for bi in range(3):
                        na = 3 if bi < 2 else 2
                        bk_ = acc_bank(ui, bi)
                        src_ = ps[bk_][:, 0:390].rearrange("p (a c) -> p a c", a=3)[:, 0:na, 0:129]
                        evac(Ost[:, 3 * bi:3 * bi + na, :], src_, [("ps", bk_)], [("Ost", bi)], eng="dve")
                    combine(og, h, 0)
                    if pending:
                        pending.pop(0)()

            def combine(og, h, aset):
                ab = og % 2
                if DBG.get('cstep', 99) < 1:
                    return
                ctx.op("dve", lambda e: e.reciprocal(out=rz[:, 0:8], in_=Ost[:, :, 128]), reads=[("Ost", 0), ("Ost", 1), ("Ost", 2)], writes=["rz0"])
                if DBG.get('cstep', 99) < 2:
                    return
                pass
                if DBG.get('cstep', 99) < 3:
                    return
                ctx.op("dve", lambda e: e.tensor_scalar(out=rz[:, 4:8], in0=rz[:, 4:8], scalar1=lam_sb[:, 1:2], scalar2=None, op0=ALU.mult),
                       reads=["rz0", "lam"], writes=["rz1b"])
                if DBG.get('cstep', 99) < 4:
                    return
                ctx.op("dve", lambda e: e.tensor_tensor(out=o1, in0=Ost[:, 0:4, 0:128], in1=rz[:, 0:4].unsqueeze(2).to_broadcast([128, 4, 128]), op=ALU.mult),
                       reads=[("Ost", 0), ("Ost", 1), "rz0"], writes=["o1"])
                if DBG.get('cstep', 99) < 5:
                    return
                ctx.op("dve", lambda e: e.tensor_tensor(out=o2, in0=Ost[:, 4:8, 0:128], in1=rz[:, 4:8].unsqueeze(2).to_broadcast([128, 4, 128]), op=ALU.mult),
                       reads=[("Ost", 1), ("Ost", 2), "rz1b"], writes=["o2"])
                if DBG.get('cstep', 99) < 6:
                    return
                ctx.op("dve", lambda e: e.tensor_tensor(out=o1, in0=o1, in1=o2, op=ALU.add), reads=["o1", "o2"], writes=["o1"])
                if DBG.get('cstep', 99) < 7:
                    return
                ctx.op("dve", lambda e: e.tensor_tensor(out=o2, in0=o1, in1=o1, op=ALU.mult), reads=["o1"], writes=["o2"])
                if DBG.get('cstep', 99) < 8:
                    return
                b = rot3.next()
                ss = ssb[b]
                if DBG.get('cstep', 99) < 9:
                    return
                ctx.op("dve", lambda e: e.tensor_reduce(out=ss[:, 0:4], in_=o2, axis=AX.X, op=ALU.add), reads=["o2"], writes=[("ss", b)])
                if DBG.get('cstep', 99) < 10:
                    return
                rs, rkey = rstd_of(ss[:, 0:4], ("ss", b), 128, 4, 1.0 / 128)
                if DBG.get('cstep', 99) < 11:
                    return
                ctx.op("pool", lambda e: e.tensor_tensor(out=o1, in0=o1, in1=rs.unsqueeze(2).to_broadcast([128, 4, 128]), op=ALU.mult),
                       reads=["o1", rkey], writes=["o1"])
                if DBG.get('cstep', 99) < 12:
                    return
                dst_ = attn_all[:, 4 * og:4 * og + 4, (4 * hp + h) * 128:(4 * hp + h + 1) * 128]
                ctx.op("pool", lambda e: e.tensor_tensor(out=dst_, in0=o1,
                                                         in1=subw.unsqueeze(1).to_broadcast([128, 4, 128]), op=ALU.mult),
                       reads=["o1", "subw"], writes=[("attn_all", og, 4 * hp + h)])

            LOOK = 2
            for k in range(min(LOOK, len(items))):
                emit_qk(k)
            for k in range(len(items)):
                emit_rest(k, mid=(lambda k=k: emit_qk(k + LOOK)) if k + LOOK < len(items) else None)
            while pending:
                pending.pop(0)()
            if hp == 1 or upto == 3:
                ctx.barrier()
            if hp == 0:
                stop(3)

        if debug:
            ctx.dma("sp", lambda e: e.dma_start(out=dbg["d_attnT"], in_=attn_all.rearrange("p a b -> p (a b)")), "dbg", reads=[])
        stop(4)

        A.off = pass_mark
        p3_mark = A.off
        A.holes = [(off_dead, dead_end)]
        h1b = [A.alloc((1024,), F32) for _ in range(2)]
        wao = A.alloc((8, 1024), BF16)
        wo = A.alloc((8, 1024), BF16)
        uT = A.alloc((4, 144), F32)
        s2 = A.alloc((4, 144), F32)
        s4 = A.alloc((4, 144), F32)
        s8 = A.alloc((4, 144), F32)
        s16 = A.alloc((144,), F32)
        pooledT = A.alloc((4, 128), BF16)
        sg_ = [A.alloc((512,), F32) for _ in range(2)]
        t1 = A.alloc((512,), F32)
        t2 = A.alloc((512,), F32)
        mrg = A.alloc((1024,), BF16)
        mrgT = A.alloc((8, 128), BF16)

        w_in_v = w_in.rearrange("(c p) n -> p c n", p=128)
        w_ao_v = w_ao.rearrange("(c p) n -> p c n", p=128)
        w_o_v = w_o.rearrange("(c p) n -> p c n", p=128)

        rot_m3 = Rot([2, 3, 4, 5, 6, 7])
        xt3 = [xt[0], xt[1]] + [A.alloc((1024,), F32) for _ in range(2)]
        xnTo = [A.alloc((8, 144), BF16) for _ in range(3)]
        poT2 = [A.alloc((4, 128), BF16) for _ in range(2)]
        aT2 = [A.alloc((8, 128), BF16) for _ in range(2)]

        def p3_A2a(i):
            xb = i % 3
            pq = i % 2
            poT, aT = poT2[pq], aT2[pq]
            pbu = rot_m3.next()
            pbu2 = rot_m3.next()
            for g in range(4):
                bank = pbu if g < 2 else pbu2
                c0 = (g % 2) * 256
                for c in range(8):
                    ctx.op("pe", lambda e, g=g, c=c, bank=bank, c0=c0: e.matmul(ps[bank][:, c0:c0 + 144], lhsT=wu_[:, c, g * 128:(g + 1) * 128],
                                                                            rhs=xnTo[xb][:, c, :], start=(c == 0), stop=(c == 7)),
                           reads=[("xnTo", xb), ("xnTo_h", xb), "wu"], writes=[("ps", bank)], inc=(c == 7))
            for bi, bank in enumerate((pbu, pbu2)):
                evac(uT[:, 2 * bi:2 * bi + 2, :], ps[bank][:].rearrange("p (a b) -> p a b", a=2)[:, :, 0:144], [("ps", bank)], [("uT", bi)])
            ukeys = [("uT", 0), ("uT", 1)]
            ctx.op("dve", lambda e: e.tensor_tensor(out=s2[:, :, 1:144], in0=uT[:, :, 1:144], in1=uT[:, :, 0:143], op=ALU.add), reads=ukeys, writes=["s2"])
            ctx.op("dve", lambda e: e.tensor_tensor(out=s4[:, 1:4, 3:144], in0=s2[:, 1:4, 3:144], in1=s2[:, 1:4, 1:142], op=ALU.add), reads=["s2"], writes=["s4"])
            ctx.op("dve", lambda e: e.tensor_tensor(out=s8[:, 2:4, 7:144], in0=s4[:, 2:4, 7:144], in1=s4[:, 2:4, 3:140], op=ALU.add), reads=["s4"], writes=["s8"])
            ctx.op("dve", lambda e: e.tensor_tensor(out=s16[:, 15:144], in0=s8[:, 3, 15:144], in1=s8[:, 3, 7:136], op=ALU.add), reads=["s8"], writes=["s16"])
            for g, (src_, skey, wdt) in enumerate(((s2[:, 0, 16:144], "s2", 2), (s4[:, 1, 16:144], "s4", 4), (s8[:, 2, 16:144], "s8", 8), (s16[:, 16:144], "s16", 16))):
                ctx.op("dve", lambda e, g=g, src_=src_, wdt=wdt: e.scalar_tensor_tensor(out=pooledT[:, g, :], in0=src_, scalar=1.0 / wdt, in1=uT[:, g, 16:144],
                                                                                 op0=ALU.mult, op1=ALU.subtract),
                       reads=[skey] + ukeys, writes=[("pooledT", g)])
            pba = rot_pT.next()
            pTa = ps[pba][:].bitcast(BF16)
            for c in range(8):
                ctx.op("pe", lambda e, c=c: e.transpose(out=pTa[:, c * 128:(c + 1) * 128], in_=attn_all[:, i, c * 128:(c + 1) * 128], identity=idb),
                       reads=["idb"], writes=[("ps", pba)], inc=(c == 7))
            evac(aT, pTa.rearrange("p (c t) -> p c t", c=8), [("ps", pba)], [("aT", pq)], eng="act")

        def p3_A2b(i):
            pq = i % 2
            poT = poT2[pq]
            pbm = rot_m3.next()
            for g in range(4):
                ctx.op("pe", lambda e, g=g: e.matmul(ps[pbm][:, g * 128:(g + 1) * 128], lhsT=gw[:, g, :], rhs=pooledT[:, g, :], start=True, stop=True),
                       reads=[("pooledT", g), "gw"], writes=[("ps", pbm)], inc=(g == 3))
            ctx.op("dve", lambda e: e.tensor_tensor(out=poT, in0=ps[pbm][:].rearrange("p (g t) -> p g t", g=4),
                                                    in1=psc.unsqueeze(2).to_broadcast([128, 4, 128]), op=ALU.mult),
                   reads=[("ps", pbm), "psc"], writes=[("poT", pq)])

        def p3_B1(i):
            xb = i % 3
            pq = i % 2
            poT, aT = poT2[pq], aT2[pq]
            for hf in range(2):
                cs = slice(hf * 512, (hf + 1) * 512)
                b_pp, b_ap, b_gp, b_ga = (rot_m3.next() for _ in range(4))
                for c in range(8):
                    ctx.op("pe", lambda e, c=c, cs=cs, b=b_gp: e.matmul(ps[b][:], lhsT=xnTo[xb][:, c, 16:144], rhs=wgt[:, c, cs], start=(c == 0), stop=(c == 7)),
                           reads=[("xnTo", xb), ("wgt", hf)], writes=[("ps", b_gp)], inc=(c == 7))
                for c in range(8):
                    ctx.op("pe", lambda e, c=c, hf=hf, b=b_ga: e.matmul(ps[b][:], lhsT=xnTo[xb][:, c, 16:144], rhs=wgt[:, c, 1024 + hf * 512:1024 + (hf + 1) * 512],
                                                                  start=(c == 0), stop=(c == 7)),
                           reads=[("xnTo", xb), ("wgt", 2 + hf)], writes=[("ps", b_ga)], inc=(c == 7))
                for c in range(8):
                    ctx.op("pe", lambda e, c=c, cs=cs, b=b_ap: e.matmul(ps[b][:], lhsT=aT[:, c, :], rhs=wao[:, c, cs], start=(c == 0), stop=(c == 7)),
                           reads=[("wao", hf), ("aT", pq)], writes=[("ps", b_ap)], inc=(c == 7))
                for g in range(4):
                    ctx.op("pe", lambda e, g=g, cs=cs, b=b_pp: e.matmul(ps[b][:], lhsT=poT[:, g, :], rhs=wpo[:, g, cs], start=(g == 0), stop=(g == 3)),
                           reads=[("poT", pq), "wpo"], writes=[("ps", b_pp)], inc=(g == 3))
                ctx.op("act", lambda e, b=b_gp: e.activation(out=sg_[0], in_=ps[b][:], func=AF.Sigmoid), reads=[("ps", b_gp)], writes=[("sg", 0)])
                ctx.op("act", lambda e, b=b_ga: e.activation(out=sg_[1], in_=ps[b][:], func=AF.Sigmoid), reads=[("ps", b_ga)], writes=[("sg", 1)])
                ctx.op("dve", lambda e, b=b_pp: e.tensor_tensor(out=t1, in0=sg_[0], in1=ps[b][:], op=ALU.mult), reads=[("sg", 0), ("ps", b_pp)], writes=["t1"])
                ctx.op("dve", lambda e, b=b_ap: e.tensor_tensor(out=t2, in0=sg_[1], in1=ps[b][:], op=ALU.mult), reads=[("sg", 1), ("ps", b_ap)], writes=["t2"])
                ctx.op("dve", lambda e, cs=cs: e.tensor_tensor(out=mrg[:, cs], in0=t1, in1=t2, op=ALU.add), reads=["t1", "t2"], writes=[("mrg", hf)])

        def p3_B2(i):
            xr = xt3[i % 4]
            xrk = ("xt", i % 4)
            hb_ = i % 2
            pb = rot_pT.next()
            pT = ps[pb][:].bitcast(BF16)
            for c in range(8):
                ctx.op("pe", lambda e, c=c: e.transpose(out=pT[:, c * 128:(c + 1) * 128], in_=mrg[:, c * 128:(c + 1) * 128], identity=idb),
                       reads=[("mrg", 0), ("mrg", 1), "idb"], writes=[("ps", pb)], inc=(c == 7))
            evac(mrgT, pT.rearrange("p (c t) -> p c t", c=8), [("ps", pb)], ["mrgT"], eng="act")
            for hf in range(2):
                cs = slice(hf * 512, (hf + 1) * 512)
                b = rot_m3.next()
                for c in range(8):
                    ctx.op("pe", lambda e, c=c, cs=cs, b=b: e.matmul(ps[b][:], lhsT=mrgT[:, c, :], rhs=wo[:, c, cs], start=(c == 0), stop=(c == 7)),
                           reads=["mrgT", ("wo", hf)], writes=[("ps", b)], inc=(c == 7))
                ctx.op("dve", lambda e, cs=cs, b=b: e.tensor_tensor(out=h1b[hb_][:, cs], in0=xr[:, cs], in1=ps[b][:], op=ALU.add),
                       reads=[xrk, ("ps", b)], writes=[("h1b", hb_, hf)])
            ctx.dma("sp", lambda e: e.dma_start(out=h1d[i * 128:(i + 1) * 128, :], in_=h1b[hb_]), "h1w_%d" % hb_,
                    reads=[("h1b", hb_, 0), ("h1b", hb_, 1)], writes=[("h1d", i)])

        l3tok = {}

        def p3_load(i):
            xb3, xb4 = i % 3, i % 4
            xbuf = xt3[xb4]
            ctx.dma("sp", lambda e: e.dma_start(out=xbuf, in_=xo[i, 16:144, :]), "x_%d" % xb4, writes=[("xt", xb4)])
            d_own = xnTo[xb3][:, :, 16:144]
            d_halo = xnTo[xb3][:, :, 0:16]
            for nm, dst_, src_, key_ in (("xlo_%d" % xb3, d_own, xnd[NSLOT + i], ("xnTo", xb3)),
                                         ("xlh_%d" % xb3, d_halo, xnd[2 * i + 1][:, :, 112:128], ("xnTo_h", xb3))):
                q_ = "pool" if nm.startswith("xlh") else "sp"
                l3tok[nm] = ctx.dma(q_, lambda e, dst_=dst_, src_=src_: e.dma_start(out=dst_, in_=src_), nm, writes=[key_],
                                    extra=[l3tok[nm]] if nm in l3tok else [])

        def consume3(i):
            p3_A2a(i)
            if i > 0:
                p3_B2(i - 1)
            p3_A2b(i)
            p3_B1(i)

        p3_load(0)
        for wdst, wsrc_, wnm in ((wao, w_ao_v, "wao"), (wo, w_o_v, "wo")):
            for hf_ in range(2):
                cs_ = slice(hf_ * 512, (hf_ + 1) * 512)
                ctx.dma("pool", lambda e, wdst=wdst, wsrc_=wsrc_, cs_=cs_: e.dma_start(out=wdst[:, :, cs_], in_=wsrc_[:, :, cs_]),
                        "w_%s%d" % (wnm, hf_), writes=[(wnm, hf_)])
        p3_load(1)
        for i in range(NOWN):
            if i + 2 < NOWN:
                p3_load(i + 2)
            consume3(i)
        p3_B2(NOWN - 1)
        ctx.barrier()
        if debug:
            ctx.dma("sp", lambda e: e.dma_start(out=dbg["d_h1"], in_=h1d), "dbg", reads=[])
        stop(5)

        A.holes = []
        A.off = m0
        wd = A.alloc((NJ, 1024), BF16)
        xr4 = [A.alloc((1024,), F32) for _ in range(2)]
        hnT = A.alloc((8, 1024), BF16)
        actT = A.alloc((NJ, 1024), BF16)
        wgb = [A.alloc((8, 256), BF16) for _ in range(2)]
        wub = [A.alloc((8, 256), BF16) for _ in range(2)]
        sil = [A.alloc((512,), BF16) for _ in range(2)]
        h2 = [A.alloc((1024,), F32) for _ in range(2)]
        yo = [A.alloc((1024,), F32) for _ in range(2)]
        fnw = A.alloc((1024,), F32)
        xt4 = [xt[0], xt[1], A.alloc((1024,), F32), A.alloc((1024,), F32)]

        ctx.dma("sp", lambda e: e.dma_start(out=nw, in_=normw[1]), "c_nw", writes=["nw"])
        ctx.dma("sp", lambda e: e.dma_start(out=fnw, in_=normw[2]), "c_fnw", writes=["fnw"])
        w_g_v = w_g.rearrange("(c p) n -> p c n", p=128)
        w_u_v = w_u.rearrange("(c p) n -> p c n", p=128)
        w_d_v = w_d.rearrange("(j p) n -> p j n", p=128)
        rot_m4 = Rot([2, 3, 4, 5, 6, 7])
        rot_w = Rot([0, 1])
        rot_sil = Rot([0, 1])
        rot_y = Rot([0, 1])
        rot_x4 = Rot([0, 1])
        def p4_jobs(th):
            jobs4 = []
            for il in range(8):
                i = th * 8 + il
                xb = il % 4
                xbuf = xt4[xb]

                def ld4(i=i, xb=xb, xbuf=xbuf):
                    ctx.dma("sp", lambda e: e.dma_start(out=xbuf, in_=h1d[i * 128:(i + 1) * 128, :]), "x_%d" % xb,
                            reads=[("h1d", i)], writes=[("xt", xb)])
                jobs4.append(dict(load=ld4, src=xbuf, src_key=("xt", xb), n=128, dst=hnT[:, :, il * 128:(il + 1) * 128], dst_key=("hnT", il)))
            return jobs4

        w_issued = {}

        def p4_issue_w(th, jb):
            wb = rot_w.next()
            cs = slice(jb * 256, (jb + 1) * 256)
            ctx.dma("pool", lambda e: e.dma_start(out=wgb[wb], in_=w_g_v[:, :, cs]), "w_g%d" % wb, writes=[("wgb", wb)])
            ctx.dma("pool", lambda e: e.dma_start(out=wub[wb], in_=w_u_v[:, :, cs]), "w_u%d" % wb, writes=[("wub", wb)])
            w_issued[(th, jb)] = wb
            return wb

        def p4_gateup(th, jbs=range(11), tgs=(0, 1)):
            for jb in jbs:
                if (th, jb) in w_issued:
                    wb = w_issued[(th, jb)]
                else:
                    wb = p4_issue_w(th, jb)
                if th == 0 and jb in (1, 2):
                    q = jb - 1
                    ctx.dma("pool", lambda e, q=q: e.dma_start(out=wd[:, q * 11:(q + 1) * 11, :], in_=w_d_v[:, q * 11:(q + 1) * 11, :]),
                            "w_wd%d" % q, writes=[("wd", q)])
                for jj in range(2):
                    j = 2 * jb + jj
                    for tg in tgs:
                        ts_ = slice(tg * 512, (tg + 1) * 512)
                        bg, bu = rot_m4.next(), rot_m4.next()
                        hkeys = [("hnT", il) for il in range(4 * tg, 4 * tg + 4)]
                        for c in range(8):
                            ctx.op("pe", lambda e, c=c, jj=jj, wb=wb, ts_=ts_, bg=bg: e.matmul(ps[bg][:], lhsT=wgb[wb][:, c, jj * 128:(jj + 1) * 128], rhs=hnT[:, c, ts_],
                                                                                        start=(c == 0), stop=(c == 7)),
                                   reads=hkeys + [("wgb", wb)], writes=[("ps", bg)], inc=(c == 7))
                        for c in range(8):
                            ctx.op("pe", lambda e, c=c, jj=jj, wb=wb, ts_=ts_, bu=bu: e.matmul(ps[bu][:], lhsT=wub[wb][:, c, jj * 128:(jj + 1) * 128], rhs=hnT[:, c, ts_],
                                                                                        start=(c == 0), stop=(c == 7)),
                                   reads=hkeys + [("wub", wb)], writes=[("ps", bu)], inc=(c == 7))
                        sb_ = rot_sil.next()
                        ctx.op("act", lambda e, sb_=sb_, bg=bg: e.activation(out=sil[sb_], in_=ps[bg][:], func=AF.Silu), reads=[("ps", bg)], writes=[("sil", sb_)])
                        ctx.op("dve", lambda e, sb_=sb_, bu=bu, j=j, ts_=ts_: e.tensor_tensor(out=actT[:, j, ts_], in0=sil[sb_], in1=ps[bu][:], op=ALU.mult),
                               reads=[("sil", sb_), ("ps", bu)], writes=[("actT", j, tg)])

        def p4_down(th, il):
            i = th * 8 + il
            yb = rot_y.next()
            xb = il % 2
            xres = xr4[xb]
            ctx.dma("sp", lambda e, i=i, xres=xres: e.dma_start(out=xres, in_=h1d[i * 128:(i + 1) * 128, :]), "xr_%d" % xb,
                    reads=[("h1d", i)], writes=[("xr", xb)])
            for hf in range(2):
                cs = slice(hf * 512, (hf + 1) * 512)
                b = rot_m4.next()
                for j in range(NJ):
                    ctx.op("pe", lambda e, j=j, cs=cs, b=b, il=il: e.matmul(ps[b][:], lhsT=actT[:, j, il * 128:(il + 1) * 128], rhs=wd[:, j, cs],
                                                                      start=(j == 0), stop=(j == NJ - 1)),
                           reads=[("actT", j, il // 4), ("wd", j // 11)], writes=[("ps", b)], inc=(j == NJ - 1))
                ctx.op("dve", lambda e, cs=cs, b=b, xb=xb, yb=yb: e.tensor_tensor(out=h2[yb][:, cs], in0=xr4[xb][:, cs], in1=ps[b][:], op=ALU.add),
                       reads=[("xr", xb), ("ps", b)], writes=[("h2", yb, hf)])
            bq = rot3.next()
            ss = ssb[bq]
            ctx.op("act", lambda e, yb=yb, ss=ss: e.activation(out=junk, in_=h2[yb], func=AF.Square, accum_out=ss[:, 0:1]),
                   reads=[("h2", yb, 0), ("h2", yb, 1)], writes=["junk", ("ss", bq)])
            rs, rkey = rstd_of(ss[:, 0:1], ("ss", bq), 128, 1, 1.0 / 1024)
            ctx.op("dve", lambda e, yb=yb, rs=rs: e.scalar_tensor_tensor(out=yo[yb], in0=h2[yb], scalar=rs, in1=fnw, op0=ALU.mult, op1=ALU.mult),
                   reads=[("h2", yb, 0), ("h2", yb, 1), rkey, "fnw"], writes=[("yo", yb)])
            out_stores.append(ctx.dma("sp", lambda e, yb=yb, i=i: e.dma_start(out=out[i * 128:(i + 1) * 128, :], in_=yo[yb]), "o_%d" % yb,
                                      reads=[("yo", yb)]))

        p4_issue_w(0, 0)

        def consume4a(k):
            if k == 3:
                p4_gateup(0, jbs=[0], tgs=[0])
            elif k == 7:
                p4_gateup(0, jbs=[0], tgs=[1])
        norm_pipe(p4_jobs(0), consume4a)
        p4_gateup(0, jbs=range(1, 11))
        p4_issue_w(1, 0)
        p4_issue_w(1, 1)
        norm_pipe(p4_jobs(1), lambda k: p4_down(0, k))
        p4_gateup(1)
        for il in range(8):
            p4_down(1, il)
        ctx.wait_only("sp", out_stores)
        ctx.barrier()
        ctx.dead = False
        print('ninst', {k: len(v) for k, v in ctx.ops.items()}, 'nsem', len(ctx.sem), 'arena peak', A.peak, flush=True)
        ctx.emit()
    return nc


def _t5_bucket(rel):
    nb = 16
    ret = np.where(rel > 0, nb, 0)
    n = np.abs(rel)
    max_exact = 8
    nf = np.maximum(n, max_exact).astype(np.float32)
    large = max_exact + (np.log(nf / np.float32(max_exact)) / np.float32(math.log(128 / max_exact)) * np.float32(nb - max_exact)).astype(np.int32)
    large = np.minimum(large, nb - 1)
    return ret + np.where(n < max_exact, n, large)


def _prep_inputs(inp):
    f = lambda a: np.ascontiguousarray(np.asarray(a, dtype=np.float32))
    x = f(inp["x"])
    meta = f(inp["meta_tokens"])
    table = f(inp["rel_bias_table"])
    B = x.shape[0]
    kl = np.arange(128)[:, None]
    ql = np.arange(128)[None, :]
    bk = np.stack([_t5_bucket(kl - ql), _t5_bucket(kl - ql - 128)], 0)
    biasT = table[bk]
    biasT = np.ascontiguousarray(biasT.transpose(1, 0, 3, 2)).reshape(128, 2 * 8 * 128)
    maskadd = np.where((kl >= 64) & (ql < 64), -30000.0, 0.0).astype(np.float32)
    shared = {
        "w_in": f(inp["w_in"][0]), "pgw": f(inp["pool_group_w"][0]), "w_po": f(inp["w_pool_out"][0]),
        "w_ao": f(inp["w_attn_out"][0]), "w_o": f(inp["w_o"][0]), "w_g": f(inp["w_gate"][0]),
        "w_u": f(inp["w_up"][0]), "w_d": f(inp["w_down"][0]),
        "normw": np.ascontiguousarray(np.broadcast_to(np.stack([f(inp["mix_norm_w"][0]), f(inp["ffn_norm_w"][0]), f(inp["final_norm_w"])], 0)[:, None, :], (3, 128, 1024))),
        "sublnw": np.ascontiguousarray(np.broadcast_to(f(inp["subln_w"][0])[None, :], (128, 128))),
        "lamv": np.ascontiguousarray(np.broadcast_to(np.concatenate([f(inp["lambda_q1"][0]), f(inp["lambda_k1"][0]), f(inp["lambda_q2"][0]), f(inp["lambda_k2"][0])])[None, :], (128, 256))),
        "cfar": np.ascontiguousarray(np.broadcast_to(table[15][None, :], (128, 8))),
        "psc": np.ascontiguousarray(f(inp["pool_scale"][0]).reshape(4, 128).T),
        "ident": np.eye(128, dtype=np.float32),
        "maskadd": maskadd,
        "biasT": biasT,
    }
    in_maps = []
    for c in range(8):
        b, p = c // 2, c % 2
        hf = np.concatenate([meta, x[b]], 0)
        idx = 128 * np.arange(NSLOT)[:, None] + np.arange(128)[None, :] - 112 - 128 * (1 - p)
        valid = (idx >= 0) & (idx < hf.shape[0])
        xk = np.where(valid.reshape(-1)[:, None], hf[np.clip(idx, 0, hf.shape[0] - 1).reshape(-1)], np.float32(0))
        xo = np.stack([hf[128 * (2 * i + p):128 * (2 * i + p) + 144] for i in range(NOWN)], 0)
        m = dict(shared)
        m["xk"] = np.ascontiguousarray(xk, dtype=np.float32)
        m["xo"] = np.ascontiguousarray(xo, dtype=np.float32)
        m["kvalid"] = np.ascontiguousarray(valid.T.astype(np.float32))
        in_maps.append(m)
    return in_maps, B


_NC_CACHE = {}


def kernel(**inputs):
    in_maps, B = _prep_inputs(inputs)
    if "nc" not in _NC_CACHE:
        _NC_CACHE["nc"] = build(0)
    res = run_bass_kernel_spmd(_NC_CACHE["nc"], in_maps, core_ids=list(range(8)))
    out = np.empty((B, 4096, 1024), np.float32)
    for c in range(8):
        b, p = c // 2, c % 2
        y = np.asarray(res.results[c]["out"], dtype=np.float32).reshape(NOWN, 128, 1024)
        for i in range(NOWN):
            g = 2 * i + p
            out[b, 128 * g:128 * (g + 1)] = y[i]
    return out
```

```python
import math
import numpy as np
from contextlib import ExitStack
import concourse.bass as bass
import concourse.mybir as mybir
from concourse.bass_utils import run_bass_kernel_spmd

F32 = mybir.dt.float32
BF16 = mybir.dt.bfloat16
ALU = mybir.AluOpType
AF = mybir.ActivationFunctionType
AX = mybir.AxisListType

NSLOT = 33
NOWN = 16
EPS = 1e-6
DFF = 2816
NJ = DFF // 128
DBG = {}


class Ctx:
    ENG = ("pe", "act", "dve", "pool", "sp")

    def __init__(self, nc, es):
        self.nc, self.es = nc, es
        self.ops = {e: [] for e in self.ENG}
        self.sem = {}
        self.val = {}
        self.waited = {}
        self.lastw = {}
        self.readers = {}
        self.maxwait = {}
        self.dead = False

    def _sem(self, key):
        if key not in self.sem:
            self.sem[key] = self.es.enter_context(self.nc.semaphore("s%d" % len(self.sem)))
            self.val[key] = 0
        return self.sem[key]

    def _deps(self, eng, reads, writes, extra=()):
        toks = list(extra)
        for k in reads:
            if k in self.lastw:
                toks.append(self.lastw[k])
        for k in writes:
            if k in self.lastw:
                toks.append(self.lastw[k])
            toks.extend(self.readers.get(k, {}).items())
        need = {}
        for (sk, v) in toks:
            if sk == "E_pe" and eng == "pe":
                continue
            if self.waited.get((eng, sk), 0) >= v:
                continue
            need[sk] = max(need.get(sk, 0), v)
        for sk, v in need.items():
            self.waited[(eng, sk)] = v
            self.maxwait[sk] = max(self.maxwait.get(sk, 0), v)
        return list(need.items())

    def _commit(self, tok, reads, writes):
        for k in reads:
            d = self.readers.setdefault(k, {})
            d[tok[0]] = max(d.get(tok[0], 0), tok[1])
        for k in writes:
            self.lastw[k] = tok
            self.readers[k] = {}

    def op(self, eng, fn, reads=(), writes=(), inc=True, extra=()):
        if self.dead:
            return None
        waits = self._deps(eng, reads, writes, extra)
        sk = "E_" + eng
        self._sem(sk)
        tok = (sk, self.val[sk] + 1)
        if inc:
            self.val[sk] += 1
        self.ops[eng].append((fn, waits, (sk, 1) if inc else None))
        self._commit(tok, reads, writes)
        return tok

    def dma(self, eng, fn, semkey, reads=(), writes=(), extra=()):
        if self.dead:
            return None
        waits = self._deps(eng, reads, writes, extra)
        self._sem(semkey)
        self.val[semkey] += 16
        tok = (semkey, self.val[semkey])
        self.ops[eng].append((fn, waits, (semkey, 16)))
        self._commit(tok, reads, writes)
        return tok

    def wait_only(self, eng, toks):
        if self.dead:
            return
        waits = self._deps(eng, (), (), toks)
        self.ops[eng].append((None, waits, None))

    def barrier(self):
        if self.dead:
            return
        toks = [(sk, v) for sk, v in self.val.items() if v > 0]
        for e in self.ENG:
            self.wait_only(e, toks)
        self.lastw = {}
        self.readers = {}

    def emit(self):
        for sk, v in self.maxwait.items():
            assert v <= self.val[sk], ("wait beyond final count", sk, v, self.val[sk])
        nc = self.nc
        with nc.Block() as block:
            def mk(name):
                def run(e):
                    for fn, waits, inc in self.ops[name]:
                        for sk, v in waits:
                            e.wait_ge(self.sem[sk], v)
                        if fn is None:
                            continue
                        ins = fn(e)
                        if inc is not None:
                            ins.then_inc(self.sem[inc[0]], inc[1])
                return run
            block.tensor(mk("pe"))
            block.scalar(mk("act"))
            block.vector(mk("dve"))
            block.gpsimd(mk("pool"))
            block.sync(mk("sp"))


class Arena:
    def __init__(self, t, nbytes):
        self.t, self.nbytes, self.off = t, nbytes, 0
        self.holes = []

    def alloc(self, shape, dt):
        n = int(np.prod(shape))
        sz = 4 if dt == F32 else 2
        self.off = (self.off + 3) // 4 * 4
        for lo, hi in self.holes:
            if self.off < hi and self.off + n * sz > lo:
                self.off = hi
        a, b = self.off // 2, (self.off + n * sz) // 2
        self.off += n * sz
        assert self.off <= self.nbytes, ("arena overflow", self.off, self.nbytes)
        self.peak = max(getattr(self, 'peak', 0), self.off)
        ap = self.t[:, a:b]
        if dt == F32:
            ap = ap.bitcast(F32)
        if len(shape) == 2:
            ap = ap.rearrange("p (a b) -> p a b", a=shape[0])
        elif len(shape) == 3:
            ap = ap.rearrange("p (a b c) -> p a b c", a=shape[0], b=shape[1])
        return ap


class Rot:
    def __init__(self, items):
        self.items, self.i = items, 0

    def next(self):
        it = self.items[self.i % len(self.items)]
        self.i += 1
        return it


def build(debug=0, upto=0):
    nc = bass.Bass("TRN2", target_bir_lowering=False)

    def din(name, shape, dt=F32):
        return nc.dram_tensor(name, shape, dt, kind="ExternalInput").ap()

    xk = din("xk", [NSLOT * 128, 1024])
    xo = din("xo", [NOWN, 144, 1024])
    w_in = din("w_in", [1024, 5632])
    pgw = din("pgw", [4, 128, 128])
    w_po = din("w_po", [512, 1024])
    w_ao = din("w_ao", [1024, 1024])
    w_o = din("w_o", [1024, 1024])
    w_g = din("w_g", [1024, DFF])
    w_u = din("w_u", [1024, DFF])
    w_d = din("w_d", [DFF, 1024])
    normw = din("normw", [3, 128, 1024])
    sublnw = din("sublnw", [128, 128])
    lamv = din("lamv", [128, 256])
    cfar_d = din("cfar", [128, 8])
    kvalid_d = din("kvalid", [128, NSLOT])
    psc_d = din("psc", [128, 4])
    ident_d = din("ident", [128, 128])
    maskadd_d = din("maskadd", [128, 128])
    biasT_d = din("biasT", [128, 2 * 8 * 128])
    out = nc.dram_tensor("out", [NOWN * 128, 1024], F32, kind="ExternalOutput").ap()
    xnd = nc.dram_tensor("xnd", [NSLOT + NOWN, 128, 8, 128], BF16).ap()
    h1d = nc.dram_tensor("h1d", [NOWN * 128, 1024], F32).ap()
    dbg = {}
    if debug:
        for nm, shp, dt in (("d_kt", [128, 4 * NSLOT * 128], BF16), ("d_vx", [128, NSLOT * 4 * 129], BF16),
                            ("d_qt", [128, 4 * 2048], BF16), ("d_attnT", [128, 8 * 2048], BF16),
                            ("d_h1", [NOWN * 128, 1024], F32)):
            dbg[nm] = nc.dram_tensor(nm, shp, dt, kind="ExternalOutput").ap()

    ARENA_BYTES = 212000
    with ExitStack() as es:
        arena_t = es.enter_context(nc.sbuf_tensor("arena", [128, ARENA_BYTES // 2], BF16))
        pall_t = es.enter_context(nc.psum_tensor("pall", [128, 4096], F32))
        pall = pall_t[:]
        ps = [pall[:, i * 512:(i + 1) * 512] for i in range(8)]
        ctx = Ctx(nc, es)
        A = Arena(arena_t, ARENA_BYTES)

        def stop(n):
            if upto == n and not ctx.dead:
                ctx.barrier()
                ctx.dead = True

        idf = A.alloc((128,), F32)
        idb = A.alloc((128,), BF16)
        nhalf = A.alloc((4,), F32)
        lam_sb = A.alloc((2,), F32)
        cfar = A.alloc((8,), F32)
        ncfar = A.alloc((8,), F32)
        kvalid = A.alloc((NSLOT,), F32)
        psc = A.alloc((4,), F32)
        subw = A.alloc((128,), F32)
        nw = A.alloc((1024,), F32)
        E = A.alloc((2, 8, 128), BF16)
        ssb = [A.alloc((4,), F32) for _ in range(3)]
        varb = [A.alloc((4,), F32) for _ in range(3)]
        rstdb = [A.alloc((4,), F32) for _ in range(3)]
        pss = [A.alloc((2,), F32) for _ in range(4)]
        pvar = [A.alloc((2,), F32) for _ in range(4)]
        prs = [A.alloc((2,), F32) for _ in range(4)]
        junk = A.alloc((1024,), BF16)
        xh = [A.alloc((1024,), BF16) for _ in range(2)]
        xt = [A.alloc((1024,), F32) for _ in range(2)]
        base_mark = A.off

        def cdma(dst, src, key):
            ctx.dma("sp", lambda e: e.dma_start(out=dst, in_=src), "c_" + key, writes=[key])

        cdma(idf, ident_d, "idf")
        cdma(cfar, cfar_d, "cfar")
        cdma(kvalid, kvalid_d, "kvalid")
        cdma(psc, psc_d, "psc")
        cdma(subw, sublnw, "subw0")
        cdma(nw, normw[0], "nw")
        ctx.op("dve", lambda e: e.tensor_copy(out=idb, in_=idf), reads=["idf"], writes=["idb"])
        ctx.op("pool", lambda e: e.memset(nhalf, -0.5), writes=["nhalf"])
        ctx.op("dve", lambda e: e.tensor_scalar(out=ncfar, in0=cfar, scalar1=-1.0, scalar2=None, op0=ALU.mult),
               reads=["cfar"], writes=["ncfar"])
        ctx.op("dve", lambda e: e.tensor_scalar(out=subw, in0=subw, scalar1=0.8, scalar2=None, op0=ALU.mult),
               reads=["subw0"], writes=["subw"])
        m0 = A.off
        lv = A.alloc((256,), F32)
        lp = A.alloc((128,), F32)
        lsum = A.alloc((2,), F32)
        lexp = A.alloc((2,), F32)
        bT = A.alloc((2, 8, 128), F32)
        madd = A.alloc((128,), F32)
        cdma(lv, lamv, "lv")
        cdma(bT.rearrange("p a b c -> p (a b c)"), biasT_d, "bT")
        cdma(madd, maskadd_d, "madd")
        def setup_lam_E():
            ctx.op("dve", lambda e: e.tensor_tensor(out=lp.rearrange("p (a b) -> p a b", a=2),
                                                    in0=lv.rearrange("p (a b c) -> p a b c", a=2, b=2)[:, :, 0, :],
                                                    in1=lv.rearrange("p (a b c) -> p a b c", a=2, b=2)[:, :, 1, :], op=ALU.mult),
                   reads=["lv"], writes=["lp"])
            ctx.op("dve", lambda e: e.tensor_reduce(out=lsum, in_=lp.rearrange("p (a b) -> p a b", a=2), axis=AX.X, op=ALU.add),
                   reads=["lp"], writes=["lsum"])
            ctx.op("act", lambda e: e.activation(out=lexp, in_=lsum, func=AF.Exp), reads=["lsum"], writes=["lexp"])
            ctx.op("dve", lambda e: e.tensor_tensor(out=lam_sb[:, 0:1], in0=lexp[:, 0:1], in1=lexp[:, 1:2], op=ALU.subtract),
                   reads=["lexp"], writes=["lam0"])
            ctx.op("dve", lambda e: e.tensor_scalar(out=lam_sb[:, 0:1], in0=lam_sb[:, 0:1], scalar1=0.2, scalar2=None, op0=ALU.add),
                   reads=["lam0"], writes=["lam1"])
            ctx.op("dve", lambda e: e.tensor_scalar(out=lam_sb[:, 1:2], in0=lam_sb[:, 0:1], scalar1=-1.0, scalar2=None, op0=ALU.mult),
                   reads=["lam1"], writes=["lam"])
            ctx.op("dve", lambda e: e.tensor_tensor(out=bT[:, 0, :, :], in0=bT[:, 0, :, :],
                                                    in1=madd.unsqueeze(1).to_broadcast([128, 8, 128]), op=ALU.add),
                   reads=["bT", "madd"], writes=["bT2"])
            for ty in range(2):
                for hh in range(8):
                    ctx.op("act", lambda e, ty=ty, hh=hh: e.activation(out=E[:, ty, hh, :], in_=bT[:, ty, hh, :], func=AF.Exp,
                                                                       bias=ncfar[:, hh:hh + 1], scale=1.0),
                           reads=["bT2", "ncfar"], writes=[("E", ty, hh)])

        A.off = m0
        stop(1)

        rot3 = Rot([0, 1, 2])
        rot_xh = Rot([0, 1])
        rot_pT = Rot([0, 1])
        evac_flip = [0]

        def evac(out_ap, in_ap, reads, writes, eng=None):
            evac_flip[0] ^= 1
            if (evac_flip[0] and eng is None) or eng == "act":
                return ctx.op("act", lambda e: e.activation(out=out_ap, in_=in_ap, func=AF.Copy), reads=reads, writes=writes)
            return ctx.op("dve", lambda e: e.tensor_copy(out=out_ap, in_=in_ap), reads=reads, writes=writes)

        def rstd_of(ss_ap, ss_key, n, k, inv_n):
            b = rot3.next()
            var, rs = varb[b], rstdb[b]
            ctx.op("pool", lambda e: e.tensor_scalar(out=var[:n, :k], in0=ss_ap, scalar1=inv_n, scalar2=EPS, op0=ALU.mult, op1=ALU.add),
                   reads=[ss_key], writes=[("var", b)])
            ctx.op("pool", lambda e: e.tensor_tensor(out=rs[:n, :k], in0=var[:n, :k], in1=nhalf[:n, :k], op=ALU.pow),
                   reads=[("var", b), "nhalf"], writes=[("rstd", b)])
            return rs[:n, :k], ("rstd", b)

        def norm_T(src, src_key, n, dst, dst_key):
            b = rot3.next()
            ss = ssb[b]
            ctx.op("act", lambda e: e.activation(out=junk[:n, :], in_=src, func=AF.Square, accum_out=ss[:n, 0:1]),
                   reads=[src_key], writes=["junk", ("ss", b)])
            rs, rkey = rstd_of(ss[:n, 0:1], ("ss", b), n, 1, 1.0 / 1024)
            hb = rot_xh.next()
            ctx.op("dve", lambda e: e.scalar_tensor_tensor(out=xh[hb][:n, :], in0=src, scalar=rs, in1=nw[:n, :], op0=ALU.mult, op1=ALU.mult),
                   reads=[src_key, rkey, "nw"], writes=[("xh", hb)])
            pb = rot_pT.next()
            pT = ps[pb][:].bitcast(BF16)
            for c in range(8):
                ctx.op("pe", lambda e, c=c: e.transpose(out=pT[:, c * 128:c * 128 + n], in_=xh[hb][:n, c * 128:(c + 1) * 128], identity=idb[:n, :n]),
                       reads=[("xh", hb), "idb"], writes=[("ps", pb)], inc=(c == 7))
            evac(dst, pT.rearrange("p (c t) -> p c t", c=8)[:, :, :n], [("ps", pb)], [dst_key])

        def norm_pipe(jobs, consume, s2off=2, xhb=None, evac_eng=None):
            xhb = xhb or xh
            nxh = len(xhb)
            N = len(jobs)

            def S0(k):
                if jobs[k].get("load") is not None:
                    jobs[k]["load"]()

            def S1(k):
                j = jobs[k]
                n, b = j["n"], k % 4
                src_, ss = j["src"], pss[b]
                ctx.op("act", lambda e: e.activation(out=junk[:n, :], in_=src_, func=AF.Square, accum_out=ss[:n, 0:1]),
                       reads=[j["src_key"]], writes=["junk", ("pss", b)])
                ctx.op("pool", lambda e: e.tensor_scalar(out=pvar[b][:n, 0:1], in0=ss[:n, 0:1], scalar1=1.0 / 1024, scalar2=EPS, op0=ALU.mult, op1=ALU.add),
                       reads=[("pss", b)], writes=[("pvar", b)])
                ctx.op("pool", lambda e: e.tensor_tensor(out=prs[b][:n, 0:1], in0=pvar[b][:n, 0:1], in1=nhalf[:n, 0:1], op=ALU.pow),
                       reads=[("pvar", b), "nhalf"], writes=[("prs", b)])

            def S2(k):
                j = jobs[k]
                n, b, hb = j["n"], k % 4, k % nxh
                src_ = j["src"]
                ctx.op("dve", lambda e: e.scalar_tensor_tensor(out=xhb[hb][:n, :], in0=src_, scalar=prs[b][:n, 0:1], in1=nw[:n, :], op0=ALU.mult, op1=ALU.mult),
                       reads=[j["src_key"], ("prs", b), "nw"], writes=[("xh", hb)])

            def S3(k):
                j = jobs[k]
                n, hb = j["n"], k % nxh
                pb = rot_pT.next()
                pT = ps[pb][:].bitcast(BF16)
                for c in range(8):
                    ctx.op("pe", lambda e, c=c: e.transpose(out=pT[:, c * 128:c * 128 + n], in_=xhb[hb][:n, c * 128:(c + 1) * 128], identity=idb[:n, :n]),
                           reads=[("xh", hb), "idb"], writes=[("ps", pb)], inc=(c == 7))
                evac(j["dst"], pT.rearrange("p (c t) -> p c t", c=8)[:, :, :n], [("ps", pb)], [j["dst_key"]], eng=evac_eng)

            for it in range(-(s2off + 2), N):
                for stage, off in ((S0, s2off + 2), (S1, s2off + 1), (S2, s2off), (S3, 1)):
                    if 0 <= it + off < N:
                        stage(it + off)
                if it >= 0:
                    consume(it)

        attn_all = A.alloc((NOWN, 1024), BF16)
        pass_mark = A.off
        rot_mm = Rot([2, 3, 4, 5])
        out_stores = []

        for hp in range(2):
            A.off = pass_mark
            KT = A.alloc((4, NSLOT * 128), BF16)
            Vx = A.alloc((NSLOT, 4, 129), BF16)
            QT = A.alloc((4, 2048), BF16)
            off_dead = (A.off + 3) // 4 * 4
            wq = A.alloc((8, 512), BF16)
            wk = A.alloc((8, 512), BF16)
            wv = A.alloc((8, 512), BF16)
            xnTg = [A.alloc((8, 512), BF16) for _ in range(3)]
            xtp = [xt[0], xt[1], A.alloc((1024,), F32)]
            PT = [A.alloc((2, 512), BF16) for _ in range(3)]
            Ost = A.alloc((8, 129), F32)
            o1 = A.alloc((4, 128), F32)
            o2 = A.alloc((4, 128), F32)
            rz = A.alloc((8,), F32)

            w_in_v = w_in.rearrange("(c p) n -> p c n", p=128)
            def load_qkv_w(hpx, defer=None):
                for nm, dst, c0 in (("wv", wv, 2560), ("wk", wk, 1536), ("wq", wq, 512)):
                    cs = slice(c0 + hpx * 512, c0 + (hpx + 1) * 512)

                    def issue(nm=nm, dst=dst, cs=cs):
                        ctx.dma("pool", lambda e: e.dma_start(out=dst, in_=w_in_v[:, :, cs]), "w_" + nm, writes=[nm])
                    if defer is None:
                        issue()
                    else:
                        defer.append(issue)
            if hp == 0:
                load_qkv_w(0)
            for h in range(4):
                ctx.op("dve", lambda e, h=h: e.tensor_copy(out=Vx[:, :, h, 128], in_=kvalid), reads=["kvalid"], writes=[("Vx1", h)])

            seq = []
            for sg in range(9):
                tl = list(range(4 * sg, min(4 * sg + 4, NSLOT)))
                for j, t in enumerate(tl):
                    seq.append(("k", sg, j, t, len(tl)))
            for og in range(4):
                for j in range(4):
                    seq.append(("q", 9 + og, j, 4 * og + j, 4))

            def compute(n):
                kind, g, j, t, nt = seq[n]
                gb = g % 3
                if kind == "k":
                    pb = rot_mm.next()
                    for c in range(8):
                        ctx.op("pe", lambda e, c=c: e.matmul(ps[pb][:], lhsT=xnTg[gb][:, c, j * 128:(j + 1) * 128], rhs=wv[:, c, :],
                                                            start=(c == 0), stop=(c == 7)),
                               reads=[("xnTg", gb, j), "wv"], writes=[("ps", pb)], inc=(c == 7))
                    evac(Vx[:, t, :, 0:128], ps[pb][:].rearrange("p (h d) -> p h d", h=4), [("ps", pb)], [("Vx", t)])
                if j != nt - 1:
                    return
                ntok = 128 * nt
                for h in range(4):
                    pbk = rot_mm.next()
                    wsrc, wkey = (wk, "wk") if kind == "k" else (wq, "wq")
                    for c in range(8):
                        ctx.op("pe", lambda e, c=c, h=h, pbk=pbk, wsrc=wsrc: e.matmul(ps[pbk][:, 0:ntok], lhsT=wsrc[:, c, h * 128:(h + 1) * 128], rhs=xnTg[gb][:, c, 0:ntok],
                                                                                 start=(c == 0), stop=(c == 7)),
                               reads=[("xnTg", gb, jj) for jj in range(nt)] + [wkey], writes=[("ps", pbk)], inc=(c == 7))
                    if kind == "k":
                        evac(KT[:, h, g * 512:g * 512 + ntok], ps[pbk][:, 0:ntok], [("ps", pbk)], [("KT", h, g)])
                    else:
                        og_ = g - 9
                        evac(QT[:, h, og_ * 512:(og_ + 1) * 512], ps[pbk][:], [("ps", pbk)], [("QT", h, og_)])

            jobs = []
            for n, (kind, g, j, t, nt) in enumerate(seq):
                xb = n % 3
                src_ap = xk[t * 128:(t + 1) * 128, :] if kind == "k" else xo[t, 16:144, :]
                xdst = xtp[xb]

                def ld(xdst=xdst, src_ap=src_ap, xb=xb):
                    ctx.dma("sp", lambda e: e.dma_start(out=xdst, in_=src_ap), "x_%d" % xb, writes=[("xt", xb)])
                jobs.append(dict(load=ld, src=xdst, src_key=("xt", xb), n=128,
                                 dst=xnTg[g % 3][:, :, j * 128:(j + 1) * 128], dst_key=("xnTg", g % 3, j)))
            xtok = {}
            if hp == 0:
                def compute_and_save(n):
                    kind, g, j, t, nt = seq[n]
                    gb = g % 3
                    tsrc = xnTg[gb][:, :, j * 128:(j + 1) * 128]
                    sk_ = "xs_%d" % (n % 4)
                    xtok[sk_] = ctx.dma("sp", lambda e: e.dma_start(out=xnd[n], in_=tsrc), sk_, reads=[("xnTg", gb, j)], writes=[("xnd", n)],
                                        extra=[xtok[sk_]] if sk_ in xtok else [])
                    compute(n)
                norm_pipe(jobs, compute_and_save)
            else:
                def load_xn(n):
                    kind, g, j, t, nt = seq[n]
                    gb = g % 3
                    tdst = xnTg[gb][:, :, j * 128:(j + 1) * 128]
                    sk_ = "xl_%d" % (n % 6)
                    xtok[sk_] = ctx.dma("sp", lambda e: e.dma_start(out=tdst, in_=xnd[n]), sk_, reads=[("xnd", n)], writes=[("xnTg", gb, j)],
                                        extra=[xtok[sk_]] if sk_ in xtok else [])
                PFX = 5
                for n in range(min(PFX, len(seq))):
                    load_xn(n)
                for n in range(len(seq)):
                    if n + PFX < len(seq):
                        load_xn(n + PFX)
                    compute(n)
            if hp == 0:
                setup_lam_E()
            pending = []
            if hp == 0:
                load_qkv_w(1, pending)
            if hp == 1:
                save_off = A.off
                A.off = off_dead
                wu_ = A.alloc((8, 512), BF16)
                wgt = A.alloc((8, 2048), BF16)
                gw = A.alloc((4, 128), BF16)
                wpo = A.alloc((4, 1024), BF16)
                dead_end = A.off
                assert dead_end <= off_dead + 3 * 8192 + 3 * 8192 + 4096
                A.off = save_off
                deadk = ["wq", "wk", "wv", ("xt", 2)] + [("xnTg", g_, j_) for g_ in range(3) for j_ in range(4)]
                pending.append(lambda: ctx.dma("pool", lambda e: e.dma_start(out=wu_, in_=w_in_v[:, :, 0:512]), "w_wu", writes=["wu"] + deadk))
                pending.append(lambda: ctx.dma("pool", lambda e: e.dma_start(out=gw, in_=pgw.rearrange("g c d -> c g d")), "w_gw", writes=["gw"] + deadk))
                for q in range(4):
                    pending.append(lambda q=q: ctx.dma("pool", lambda e: e.dma_start(out=wgt[:, :, q * 512:(q + 1) * 512], in_=w_in_v[:, :, 3584 + q * 512:3584 + (q + 1) * 512]),
                                                      "w_wgt%d" % q, writes=[("wgt", q)] + deadk))
                pending.append(lambda: ctx.dma("pool", lambda e: e.dma_start(out=wpo, in_=w_po.rearrange("(g p) n -> p g n", p=128)), "w_wpo", writes=["wpo"] + deadk))

            if debug and hp == 0:
                ctx.dma("sp", lambda e: e.dma_start(out=dbg["d_kt"], in_=KT.rearrange("p a b -> p (a b)")), "dbg",
                        reads=[("KT", h, sg) for h in range(4) for sg in range(9)])
                ctx.dma("sp", lambda e: e.dma_start(out=dbg["d_vx"], in_=Vx.rearrange("p a b c -> p (a b c)")), "dbg",
                        reads=[("Vx", t) for t in range(NSLOT)] + [("Vx1", h) for h in range(4)])
                ctx.dma("sp", lambda e: e.dma_start(out=dbg["d_qt"], in_=QT.rearrange("p a b -> p (a b)")), "dbg",
                        reads=[("QT", h, og) for h in range(4) for og in range(4)])

            if hp == 0:
                stop(2)
            rot_s = Rot([0, 2])
            rot_pt = Rot([0, 1, 2])
            units = [(og, h) for og in range(4) for h in range(4)]
            items = []
            for ui, (og, h) in enumerate(units):
                i0 = 4 * og
                for s in range(2 * i0 + 9):
                    items.append((ui, s))

            def acc_bank(ui, bi):
                return 4 + ((-ui) % 4 + bi) % 4

            def acc_ap(ui, a):
                bank = acc_bank(ui, a // 3)
                c0 = (a % 3) * 130
                return ps[bank][:, c0:c0 + 129], ("ps", bank)

            sbank = {}

            def emit_qk(k):
                ui, s = items[k]
                og, h = units[ui]
                i0 = 4 * og
                jmin = max(0, (s - 2 * i0 - 2 + 1) // 2)
                ncol = (4 - jmin) * 128
                sb_ = rot_s.next()
                sbank[k] = sb_
                q0 = og * 512 + jmin * 128
                for m in range(2):
                    ctx.op("pe", lambda e, m=m: e.matmul(ps[sb_ + m][:, 0:ncol], lhsT=KT[m * 64:(m + 1) * 64, h, s * 128:(s + 1) * 128],
                                                         rhs=QT[m * 64:(m + 1) * 64, h, q0:q0 + ncol], start=True, stop=True),
                           reads=[("KT", h, s // 4), ("QT", h, og)], writes=[("ps", sb_ + m)], inc=(m == 1))

            def emit_rest(k, mid=None):
                ui, s = items[k]
                og, h = units[ui]
                hh = 4 * hp + h
                i0 = 4 * og
                nsl = 2 * i0 + 9
                jmin = max(0, (s - 2 * i0 - 2 + 1) // 2)
                ncol = (4 - jmin) * 128
                sb_ = sbank.pop(k)
                pb = rot_pt.next()
                s_in = pall[:, sb_ * 512:(sb_ + 2) * 512].rearrange("p (m c) -> p m c", m=2)[:, :, 0:ncol]
                ctx.op("act", lambda e: e.activation(out=PT[pb][:, :, 0:ncol], in_=s_in, func=AF.Exp,
                                                     bias=cfar[:, hh:hh + 1], scale=0.125),
                       reads=[("ps", sb_), ("ps", sb_ + 1), "cfar"], writes=[("PT", pb, jj) for jj in range(4)])
                d = s - (2 * i0 + 2)
                if d >= -1:
                    if d % 2 == 0:
                        jf, ty = d // 2, 0
                    else:
                        jf, ty = (d + 1) // 2, 1
                    cf = (jf - jmin) * 128
                    ctx.op("dve", lambda e: e.tensor_tensor(out=PT[pb][:, :, cf:cf + 128], in0=PT[pb][:, :, cf:cf + 128],
                                                             in1=E[:, ty, hh, :].unsqueeze(1).to_broadcast([128, 2, 128]), op=ALU.mult),
                           reads=[("PT", pb, jf - jmin), ("E", ty, hh)], writes=[("PT", pb, jf - jmin)])
                if mid is not None:
                    mid()
                jfix = jf if d >= -1 else -1
                pv = [(m, j) for m in range(2) for j in range(jmin, 4) if j != jfix] + [(m, j) for m in range(2) for j in range(jmin, 4) if j == jfix]
                for idx, (m, j) in enumerate(pv):
                    a = m * 4 + j
                    acc, akey = acc_ap(ui, a)
                    first = (s == 0 and a % 3 == 0)
                    last = (s == 2 * (i0 + j) + 2)
                    cj = (j - jmin) * 128
                    ctx.op("pe", lambda e, acc=acc, cj=cj, m=m, first=first, last=last: e.matmul(acc, lhsT=PT[pb][:, m, cj:cj + 128], rhs=Vx[:, s, h, :],
                                                                                           start=first, stop=last, skip_group_check=True),
                           reads=[("PT", pb, j - jmin), ("Vx", s), ("Vx1", h)], writes=[akey], inc=(idx == len(pv) - 1))
                if s == nsl - 1:
                    for bi in range(3):
                        na = 3 if bi < 2 else 2
                        bk_ = acc_bank(ui, bi)
                        src_ = ps[bk_][:, 0:390].rearrange("p (a c) -> p a c", a=3)[:, 0:na, 0:129]
                        evac(Ost[:, 3 * bi:3 * bi + na, :], src_, [("ps", bk_)], [("Ost", bi)], eng="dve")
                    combine(og, h, 0)
                    if pending:
                        pending.pop(0)()

            def combine(og, h, aset):
                ab = og % 2
                if DBG.get('cstep', 99) < 1:
                    return
                ctx.op("dve", lambda e: e.reciprocal(out=rz[:, 0:8], in_=Ost[:, :, 128]), reads=[("Ost", 0), ("Ost", 1), ("Ost", 2)], writes=["rz0"])
                if DBG.get('cstep', 99) < 2:
                    return
                pass
                if DBG.get('cstep', 99) < 3:
                    return
                ctx.op("dve", lambda e: e.tensor_scalar(out=rz[:, 4:8], in0=rz[:, 4:8], scalar1=lam_sb[:, 1:2], scalar2=None, op0=ALU.mult),
                       reads=["rz0", "lam"], writes=["rz1b"])
                if DBG.get('cstep', 99) < 4:
                    return
                ctx.op("dve", lambda e: e.tensor_tensor(out=o1, in0=Ost[:, 0:4, 0:128], in1=rz[:, 0:4].unsqueeze(2).to_broadcast([128, 4, 128]), op=ALU.mult),
                       reads=[("Ost", 0), ("Ost", 1), "rz0"], writes=["o1"])
                if DBG.get('cstep', 99) < 5:
                    return
                ctx.op("dve", lambda e: e.tensor_tensor(out=o2, in0=Ost[:, 4:8, 0:128], in1=rz[:, 4:8].unsqueeze(2).to_broadcast([128, 4, 128]), op=ALU.mult),
                       reads=[("Ost", 1), ("Ost", 2), "rz1b"], writes=["o2"])
                if DBG.get('cstep', 99) < 6:
                    return
                ctx.op("dve", lambda e: e.tensor_tensor(out=o1, in0=o1, in1=o2, op=ALU.add), reads=["o1", "o2"], writes=["o1"])
                if DBG.get('cstep', 99) < 7:
                    return
                ctx.op("dve", lambda e: e.tensor_tensor(out=o2, in0=o1, in1=o1, op=ALU.mult), reads=["o1"], writes=["o2"])
                if DBG.get('cstep', 99) < 8:
                    return
                b = rot3.next()
                ss = ssb[b]
                if DBG.get('cstep', 99) < 9:
                    return
                ctx.op("dve", lambda e: e.tensor_reduce(out=ss[:, 0:4], in_=o2, axis=AX.X, op=ALU.add), reads=["o2"], writes=[("ss", b)])
                if DBG.get('cstep', 99) < 10:
                    return
                rs, rkey = rstd_of(ss[:, 0:4], ("ss", b), 128, 4, 1.0 / 128)
                if DBG.get('cstep', 99) < 11:
                    return
                ctx.op("pool", lambda e: e.tensor_tensor(out=o1, in0=o1, in1=rs.unsqueeze(2).to_broadcast([128, 4, 128]), op=ALU.mult),
                       reads=["o1", rkey], writes=["o1"])
                if DBG.get('cstep', 99) < 12:
                    return
                dst_ = attn_all[:, 4 * og:4 * og + 4, (4 * hp + h) * 128:(4 * hp + h + 1) * 128]
                ctx.op("pool", lambda e: e.tensor_tensor(out=dst_, in0=o1,
                                                         in1=subw.unsqueeze(1).to_broadcast([128, 4, 128]), op=ALU.mult),
                       reads=["o1", "subw"], writes=[("attn_all", og, 4 * hp + h)])

            LOOK = 2
            for k in range(min(LOOK, len(items))):
                emit_qk(k)
            for k in range(len(items)):
                emit_rest(k, mid=(lambda k=k: emit_qk(k + LOOK)) if k + LOOK < len(items) else None)
            while pending:
                pending.pop(0)()
            if hp == 1 or upto == 3:
                ctx.barrier()
            if hp == 0:
                stop(3)

        if debug:
            ctx.dma("sp", lambda e: e.dma_start(out=dbg["d_attnT"], in_=attn_all.rearrange("p a b -> p (a b)")), "dbg", reads=[])
        stop(4)

        A.off = pass_mark
        p3_mark = A.off
        A.holes = [(off_dead, dead_end)]
        h1b = [A.alloc((1024,), F32) for _ in range(2)]
        wao = A.alloc((8, 1024), BF16)
        wo = A.alloc((8, 1024), BF16)
        uT = A.alloc((4, 144), F32)
        s2 = A.alloc((4, 144), F32)
        s4 = A.alloc((4, 144), F32)
        s8 = A.alloc((4, 144), F32)
        s16 = A.alloc((144,), F32)
        pooledT = A.alloc((4, 128), BF16)
        sg_ = [A.alloc((512,), F32) for _ in range(2)]
        t1 = A.alloc((512,), F32)
        t2 = A.alloc((512,), F32)
        mrg = A.alloc((1024,), BF16)
        mrgT = A.alloc((8, 128), BF16)

        w_in_v = w_in.rearrange("(c p) n -> p c n", p=128)
        w_ao_v = w_ao.rearrange("(c p) n -> p c n", p=128)
        w_o_v = w_o.rearrange("(c p) n -> p c n", p=128)

        rot_m3 = Rot([2, 3, 4, 5, 6, 7])
        xt3 = [xt[0], xt[1]] + [A.alloc((1024,), F32) for _ in range(2)]
        xnTo = [A.alloc((8, 144), BF16) for _ in range(3)]
        poT2 = [A.alloc((4, 128), BF16) for _ in range(2)]
        aT2 = [A.alloc((8, 128), BF16) for _ in range(2)]

        def p3_A2a(i):
            xb = i % 3
            pq = i % 2
            poT, aT = poT2[pq], aT2[pq]
            pbu = rot_m3.next()
            pbu2 = rot_m3.next()
            for g in range(4):
                bank = pbu if g < 2 else pbu2
                c0 = (g % 2) * 256
                for c in range(8):
                    ctx.op("pe", lambda e, g=g, c=c, bank=bank, c0=c0: e.matmul(ps[bank][:, c0:c0 + 144], lhsT=wu_[:, c, g * 128:(g + 1) * 128],
                                                                            rhs=xnTo[xb][:, c, :], start=(c == 0), stop=(c == 7)),
                           reads=[("xnTo", xb), ("xnTo_h", xb), "wu"], writes=[("ps", bank)], inc=(c == 7))
            for bi, bank in enumerate((pbu, pbu2)):
                evac(uT[:, 2 * bi:2 * bi + 2, :], ps[bank][:].rearrange("p (a b) -> p a b", a=2)[:, :, 0:144], [("ps", bank)], [("uT", bi)])
            ukeys = [("uT", 0), ("uT", 1)]
            ctx.op("dve", lambda e: e.tensor_tensor(out=s2[:, :, 1:144], in0=uT[:, :, 1:144], in1=uT[:, :, 0:143], op=ALU.add), reads=ukeys, writes=["s2"])
            ctx.op("dve", lambda e: e.tensor_tensor(out=s4[:, 1:4, 3:144], in0=s2[:, 1:4, 3:144], in1=s2[:, 1:4, 1:142], op=ALU.add), reads=["s2"], writes=["s4"])
            ctx.op("dve", lambda e: e.tensor_tensor(out=s8[:, 2:4, 7:144], in0=s4[:, 2:4, 7:144], in1=s4[:, 2:4, 3:140], op=ALU.add), reads=["s4"], writes=["s8"])
            ctx.op("dve", lambda e: e.tensor_tensor(out=s16[:, 15:144], in0=s8[:, 3, 15:144], in1=s8[:, 3, 7:136], op=ALU.add), reads=["s8"], writes=["s16"])
            for g, (src_, skey, wdt) in enumerate(((s2[:, 0, 16:144], "s2", 2), (s4[:, 1, 16:144], "s4", 4), (s8[:, 2, 16:144], "s8", 8), (s16[:, 16:144], "s16", 16))):
                ctx.op("dve", lambda e, g=g, src_=src_, wdt=wdt: e.scalar_tensor_tensor(out=pooledT[:, g, :], in0=src_, scalar=1.0 / wdt, in1=uT[:, g, 16:144],
                                                                                 op0=ALU.mult, op1=ALU.subtract),
                       reads=[skey] + ukeys, writes=[("pooledT", g)])
            pba = rot_pT.next()
            pTa = ps[pba][:].bitcast(BF16)
            for c in range(8):
                ctx.op("pe", lambda e, c=c: e.transpose(out=pTa[:, c * 128:(c + 1) * 128], in_=attn_all[:, i, c * 128:(c + 1) * 128], identity=idb),
                       reads=["idb"], writes=[("ps", pba)], inc=(c == 7))
            evac(aT, pTa.rearrange("p (c t) -> p c t", c=8), [("ps", pba)], [("aT", pq)], eng="act")

        def p3_A2b(i):
            pq = i % 2
            poT = poT2[pq]
            pbm = rot_m3.next()
            for g in range(4):
                ctx.op("pe", lambda e, g=g: e.matmul(ps[pbm][:, g * 128:(g + 1) * 128], lhsT=gw[:, g, :], rhs=pooledT[:, g, :], start=True, stop=True),
                       reads=[("pooledT", g), "gw"], writes=[("ps", pbm)], inc=(g == 3))
            ctx.op("dve", lambda e: e.tensor_tensor(out=poT, in0=ps[pbm][:].rearrange("p (g t) -> p g t", g=4),
                                                    in1=psc.unsqueeze(2).to_broadcast([128, 4, 128]), op=ALU.mult),
                   reads=[("ps", pbm), "psc"], writes=[("poT", pq)])

        def p3_B1(i):
            xb = i % 3
            pq = i % 2
            poT, aT = poT2[pq], aT2[pq]
            for hf in range(2):
                cs = slice(hf * 512, (hf + 1) * 512)
                b_pp, b_ap, b_gp, b_ga = (rot_m3.next() for _ in range(4))
                for c in range(8):
                    ctx.op("pe", lambda e, c=c, cs=cs, b=b_gp: e.matmul(ps[b][:], lhsT=xnTo[xb][:, c, 16:144], rhs=wgt[:, c, cs], start=(c == 0), stop=(c == 7)),
                           reads=[("xnTo", xb), ("wgt", hf)], writes=[("ps", b_gp)], inc=(c == 7))
                for c in range(8):
                    ctx.op("pe", lambda e, c=c, hf=hf, b=b_ga: e.matmul(ps[b][:], lhsT=xnTo[xb][:, c, 16:144], rhs=wgt[:, c, 1024 + hf * 512:1024 + (hf + 1) * 512],
                                                                  start=(c == 0), stop=(c == 7)),
                           reads=[("xnTo", xb), ("wgt", 2 + hf)], writes=[("ps", b_ga)], inc=(c == 7))
                for c in range(8):
                    ctx.op("pe", lambda e, c=c, cs=cs, b=b_ap: e.matmul(ps[b][:], lhsT=aT[:, c, :], rhs=wao[:, c, cs], start=(c == 0), stop=(c == 7)),
                           reads=[("wao", hf), ("aT", pq)], writes=[("ps", b_ap)], inc=(c == 7))
                for g in range(4):
                    ctx.op("pe", lambda e, g=g, cs=cs, b=b_pp: e.matmul(ps[b][:], lhsT=poT[:, g, :], rhs=wpo[:, g, cs], start=(g == 0), stop=(g == 3)),
                           reads=[("poT", pq), "wpo"], writes=[("ps", b_pp)], inc=(g == 3))
                ctx.op("act", lambda e, b=b_gp: e.activation(out=sg_[0], in_=ps[b][:], func=AF.Sigmoid), reads=[("ps", b_gp)], writes=[("sg", 0)])
                ctx.op("act", lambda e, b=b_ga: e.activation(out=sg_[1], in_=ps[b][:], func=AF.Sigmoid), reads=[("ps", b_ga)], writes=[("sg", 1)])
                ctx.op("dve", lambda e, b=b_pp: e.tensor_tensor(out=t1, in0=sg_[0], in1=ps[b][:], op=ALU.mult), reads=[("sg", 0), ("ps", b_pp)], writes=["t1"])
                ctx.op("dve", lambda e, b=b_ap: e.tensor_tensor(out=t2, in0=sg_[1], in1=ps[b][:], op=ALU.mult), reads=[("sg", 1), ("ps", b_ap)], writes=["t2"])
                ctx.op("dve", lambda e, cs=cs: e.tensor_tensor(out=mrg[:, cs], in0=t1, in1=t2, op=ALU.add), reads=["t1", "t2"], writes=[("mrg", hf)])

        def p3_B2(i):
            xr = xt3[i % 4]
            xrk = ("xt", i % 4)
            hb_ = i % 2
            pb = rot_pT.next()
            pT = ps[pb][:].bitcast(BF16)
            for c in range(8):
                ctx.op("pe", lambda e, c=c: e.transpose(out=pT[:, c * 128:(c + 1) * 128], in_=mrg[:, c * 128:(c + 1) * 128], identity=idb),
                       reads=[("mrg", 0), ("mrg", 1), "idb"], writes=[("ps", pb)], inc=(c == 7))
            evac(mrgT, pT.rearrange("p (c t) -> p c t", c=8), [("ps", pb)], ["mrgT"], eng="act")
            for hf in range(2):
                cs = slice(hf * 512, (hf + 1) * 512)
                b = rot_m3.next()
                for c in range(8):
                    ctx.op("pe", lambda e, c=c, cs=cs, b=b: e.matmul(ps[b][:], lhsT=mrgT[:, c, :], rhs=wo[:, c, cs], start=(c == 0), stop=(c == 7)),
                           reads=["mrgT", ("wo", hf)], writes=[("ps", b)], inc=(c == 7))
                ctx.op("dve", lambda e, cs=cs, b=b: e.tensor_tensor(out=h1b[hb_][:, cs], in0=xr[:, cs], in1=ps[b][:], op=ALU.add),
                       reads=[xrk, ("ps", b)], writes=[("h1b", hb_, hf)])
            ctx.dma("sp", lambda e: e.dma_start(out=h1d[i * 128:(i + 1) * 128, :], in_=h1b[hb_]), "h1w_%d" % hb_,
                    reads=[("h1b", hb_, 0), ("h1b", hb_, 1)], writes=[("h1d", i)])

        l3tok = {}

        def p3_load(i):
            xb3, xb4 = i % 3, i % 4
            xbuf = xt3[xb4]
            ctx.dma("sp", lambda e: e.dma_start(out=xbuf, in_=xo[i, 16:144, :]), "x_%d" % xb4, writes=[("xt", xb4)])
            d_own = xnTo[xb3][:, :, 16:144]
            d_halo = xnTo[xb3][:, :, 0:16]
            for nm, dst_, src_, key_ in (("xlo_%d" % xb3, d_own, xnd[NSLOT + i], ("xnTo", xb3)),
                                         ("xlh_%d" % xb3, d_halo, xnd[2 * i + 1][:, :, 112:128], ("xnTo_h", xb3))):
                q_ = "pool" if nm.startswith("xlh") else "sp"
                l3tok[nm] = ctx.dma(q_, lambda e, dst_=dst_, src_=src_: e.dma_start(out=dst_, in_=src_), nm, writes=[key_],
                                    extra=[l3tok[nm]] if nm in l3tok else [])

        def consume3(i):
            p3_A2a(i)
            if i > 0:
                p3_B2(i - 1)
            p3_A2b(i)
            p3_B1(i)

        p3_load(0)
        for wdst, wsrc_, wnm in ((wao, w_ao_v, "wao"), (wo, w_o_v, "wo")):
            for hf_ in range(2):
                cs_ = slice(hf_ * 512, (hf_ + 1) * 512)
                ctx.dma("pool", lambda e, wdst=wdst, wsrc_=wsrc_, cs_=cs_: e.dma_start(out=wdst[:, :, cs_], in_=wsrc_[:, :, cs_]),
                        "w_%s%d" % (wnm, hf_), writes=[(wnm, hf_)])
        p3_load(1)
        ctx.dma("sp", lambda e: e.dma_start(out=nw, in_=normw[1]), "c_nw", writes=["nw"])
        for i in range(NOWN):
            if i + 2 < NOWN:
                p3_load(i + 2)
            consume3(i)
        p3_B2(NOWN - 1)
        ctx.barrier()
        if debug:
            ctx.dma("sp", lambda e: e.dma_start(out=dbg["d_h1"], in_=h1d), "dbg", reads=[])
        stop(5)

        A.holes = []
        A.off = m0
        wd = A.alloc((NJ, 1024), BF16)
        xr4 = [A.alloc((1024,), F32) for _ in range(2)]
        hnT = A.alloc((8, 1024), BF16)
        actT = A.alloc((NJ, 1024), BF16)
        wgb = [A.alloc((8, 256), BF16) for _ in range(2)]
        wub = [A.alloc((8, 256), BF16) for _ in range(2)]
        sil = [A.alloc((512,), BF16) for _ in range(2)]
        h2 = [A.alloc((1024,), F32) for _ in range(2)]
        yo = [A.alloc((1024,), F32) for _ in range(2)]
        fnw = A.alloc((1024,), F32)
        xt4 = [xt[0], xt[1], A.alloc((1024,), F32), A.alloc((1024,), F32)]

        ctx.dma("sp", lambda e: e.dma_start(out=fnw, in_=normw[2]), "c_fnw", writes=["fnw"])
        w_g_v = w_g.rearrange("(c p) n -> p c n", p=128)
        w_u_v = w_u.rearrange("(c p) n -> p c n", p=128)
        w_d_v = w_d.rearrange("(j p) n -> p j n", p=128)
        rot_m4 = Rot([2, 3, 4, 5, 6, 7])
        rot_w = Rot([0, 1])
        rot_sil = Rot([0, 1])
        rot_y = Rot([0, 1])
        rot_x4 = Rot([0, 1])
        def p4_jobs(th):
            jobs4 = []
            for il in range(8):
                i = th * 8 + il
                xb = il % 4
                xbuf = xt4[xb]

                def ld4(i=i, xb=xb, xbuf=xbuf):
                    ctx.dma("sp", lambda e: e.dma_start(out=xbuf, in_=h1d[i * 128:(i + 1) * 128, :]), "x_%d" % xb,
                            reads=[("h1d", i)], writes=[("xt", xb)])
                jobs4.append(dict(load=ld4, src=xbuf, src_key=("xt", xb), n=128, dst=hnT[:, :, il * 128:(il + 1) * 128], dst_key=("hnT", il)))
            return jobs4

        w_issued = {}

        def p4_issue_w(th, jb):
            wb = rot_w.next()
            cs = slice(jb * 256, (jb + 1) * 256)
            ctx.dma("pool", lambda e: e.dma_start(out=wgb[wb], in_=w_g_v[:, :, cs]), "w_g%d" % wb, writes=[("wgb", wb)])
            ctx.dma("pool", lambda e: e.dma_start(out=wub[wb], in_=w_u_v[:, :, cs]), "w_u%d" % wb, writes=[("wub", wb)])
            w_issued[(th, jb)] = wb
            return wb

        def p4_gateup(th, jbs=range(11), tgs=(0, 1)):
            for jb in jbs:
                if (th, jb) in w_issued:
                    wb = w_issued[(th, jb)]
                else:
                    wb = p4_issue_w(th, jb)
                if th == 0 and jb in (1, 2):
                    q = jb - 1
                    ctx.dma("pool", lambda e, q=q: e.dma_start(out=wd[:, q * 11:(q + 1) * 11, :], in_=w_d_v[:, q * 11:(q + 1) * 11, :]),
                            "w_wd%d" % q, writes=[("wd", q)])
                for jj in range(2):
                    j = 2 * jb + jj
                    for tg in tgs:
                        ts_ = slice(tg * 512, (tg + 1) * 512)
                        bg, bu = rot_m4.next(), rot_m4.next()
                        hkeys = [("hnT", il) for il in range(4 * tg, 4 * tg + 4)]
                        for c in range(8):
                            ctx.op("pe", lambda e, c=c, jj=jj, wb=wb, ts_=ts_, bg=bg: e.matmul(ps[bg][:], lhsT=wgb[wb][:, c, jj * 128:(jj + 1) * 128], rhs=hnT[:, c, ts_],
                                                                                        start=(c == 0), stop=(c == 7)),
                                   reads=hkeys + [("wgb", wb)], writes=[("ps", bg)], inc=(c == 7))
                        for c in range(8):
                            ctx.op("pe", lambda e, c=c, jj=jj, wb=wb, ts_=ts_, bu=bu: e.matmul(ps[bu][:], lhsT=wub[wb][:, c, jj * 128:(jj + 1) * 128], rhs=hnT[:, c, ts_],
                                                                                        start=(c == 0), stop=(c == 7)),
                                   reads=hkeys + [("wub", wb)], writes=[("ps", bu)], inc=(c == 7))
                        sb_ = rot_sil.next()
                        ctx.op("act", lambda e, sb_=sb_, bg=bg: e.activation(out=sil[sb_], in_=ps[bg][:], func=AF.Silu), reads=[("ps", bg)], writes=[("sil", sb_)])
                        ctx.op("dve", lambda e, sb_=sb_, bu=bu, j=j, ts_=ts_: e.tensor_tensor(out=actT[:, j, ts_], in0=sil[sb_], in1=ps[bu][:], op=ALU.mult),
                               reads=[("sil", sb_), ("ps", bu)], writes=[("actT", j, tg)])

        def p4_down(th, il):
            i = th * 8 + il
            yb = rot_y.next()
            xb = il % 2
            xres = xr4[xb]
            ctx.dma("sp", lambda e, i=i, xres=xres: e.dma_start(out=xres, in_=h1d[i * 128:(i + 1) * 128, :]), "xr_%d" % xb,
                    reads=[("h1d", i)], writes=[("xr", xb)])
            for hf in range(2):
                cs = slice(hf * 512, (hf + 1) * 512)
                b = rot_m4.next()
                for j in range(NJ):
                    ctx.op("pe", lambda e, j=j, cs=cs, b=b, il=il: e.matmul(ps[b][:], lhsT=actT[:, j, il * 128:(il + 1) * 128], rhs=wd[:, j, cs],
                                                                      start=(j == 0), stop=(j == NJ - 1)),
                           reads=[("actT", j, il // 4), ("wd", j // 11)], writes=[("ps", b)], inc=(j == NJ - 1))
                ctx.op("dve", lambda e, cs=cs, b=b, xb=xb, yb=yb: e.tensor_tensor(out=h2[yb][:, cs], in0=xr4[xb][:, cs], in1=ps[b][:], op=ALU.add),
                       reads=[("xr", xb), ("ps", b)], writes=[("h2", yb, hf)])
            bq = rot3.next()
            ss = ssb[bq]
            ctx.op("act", lambda e, yb=yb, ss=ss: e.activation(out=junk, in_=h2[yb], func=AF.Square, accum_out=ss[:, 0:1]),
                   reads=[("h2", yb, 0), ("h2", yb, 1)], writes=["junk", ("ss", bq)])
            rs, rkey = rstd_of(ss[:, 0:1], ("ss", bq), 128, 1, 1.0 / 1024)
            ctx.op("dve", lambda e, yb=yb, rs=rs: e.scalar_tensor_tensor(out=yo[yb], in0=h2[yb], scalar=rs, in1=fnw, op0=ALU.mult, op1=ALU.mult),
                   reads=[("h2", yb, 0), ("h2", yb, 1), rkey, "fnw"], writes=[("yo", yb)])
            out_stores.append(ctx.dma("sp", lambda e, yb=yb, i=i: e.dma_start(out=out[i * 128:(i + 1) * 128, :], in_=yo[yb]), "o_%d" % yb,
                                      reads=[("yo", yb)]))

        p4_issue_w(0, 0)

        def consume4a(k):
            if k == 3:
                p4_gateup(0, jbs=[0], tgs=[0])
            elif k == 7:
                p4_gateup(0, jbs=[0], tgs=[1])
        norm_pipe(p4_jobs(0), consume4a)
        p4_gateup(0, jbs=range(1, 11))
        p4_issue_w(1, 0)
        p4_issue_w(1, 1)
        norm_pipe(p4_jobs(1), lambda k: p4_down(0, k))
        p4_gateup(1)
        for il in range(8):
            p4_down(1, il)
        ctx.wait_only("sp", out_stores)
        ctx.barrier()
        ctx.dead = False
        print('ninst', {k: len(v) for k, v in ctx.ops.items()}, 'nsem', len(ctx.sem), 'arena peak', A.peak, flush=True)
        ctx.emit()
    return nc


def _t5_bucket(rel):
    nb = 16
    ret = np.where(rel > 0, nb, 0)
    n = np.abs(rel)
    max_exact = 8
    nf = np.maximum(n, max_exact).astype(np.float32)
    large = max_exact + (np.log(nf / np.float32(max_exact)) / np.float32(math.log(128 / max_exact)) * np.float32(nb - max_exact)).astype(np.int32)
    large = np.minimum(large, nb - 1)
    return ret + np.where(n < max_exact, n, large)


def _prep_inputs(inp):
    f = lambda a: np.ascontiguousarray(np.asarray(a, dtype=np.float32))
    x = f(inp["x"])
    meta = f(inp["meta_tokens"])
    table = f(inp["rel_bias_table"])
    B = x.shape[0]
    kl = np.arange(128)[:, None]
    ql = np.arange(128)[None, :]
    bk = np.stack([_t5_bucket(kl - ql), _t5_bucket(kl - ql - 128)], 0)
    biasT = table[bk]
    biasT = np.ascontiguousarray(biasT.transpose(1, 0, 3, 2)).reshape(128, 2 * 8 * 128)
    maskadd = np.where((kl >= 64) & (ql < 64), -30000.0, 0.0).astype(np.float32)
    shared = {
        "w_in": f(inp["w_in"][0]), "pgw": f(inp["pool_group_w"][0]), "w_po": f(inp["w_pool_out"][0]),
        "w_ao": f(inp["w_attn_out"][0]), "w_o": f(inp["w_o"][0]), "w_g": f(inp["w_gate"][0]),
        "w_u": f(inp["w_up"][0]), "w_d": f(inp["w_down"][0]),
        "normw": np.ascontiguousarray(np.broadcast_to(np.stack([f(inp["mix_norm_w"][0]), f(inp["ffn_norm_w"][0]), f(inp["final_norm_w"])], 0)[:, None, :], (3, 128, 1024))),
        "sublnw": np.ascontiguousarray(np.broadcast_to(f(inp["subln_w"][0])[None, :], (128, 128))),
        "lamv": np.ascontiguousarray(np.broadcast_to(np.concatenate([f(inp["lambda_q1"][0]), f(inp["lambda_k1"][0]), f(inp["lambda_q2"][0]), f(inp["lambda_k2"][0])])[None, :], (128, 256))),
        "cfar": np.ascontiguousarray(np.broadcast_to(table[15][None, :], (128, 8))),
        "psc": np.ascontiguousarray(f(inp["pool_scale"][0]).reshape(4, 128).T),
        "ident": np.eye(128, dtype=np.float32),
        "maskadd": maskadd,
        "biasT": biasT,
    }
    in_maps = []
    for c in range(8):
        b, p = c // 2, c % 2
        hf = np.concatenate([meta, x[b]], 0)
        idx = 128 * np.arange(NSLOT)[:, None] + np.arange(128)[None, :] - 112 - 128 * (1 - p)
        valid = (idx >= 0) & (idx < hf.shape[0])
        xk = np.where(valid.reshape(-1)[:, None], hf[np.clip(idx, 0, hf.shape[0] - 1).reshape(-1)], np.float32(0))
        xo = np.stack([hf[128 * (2 * i + p):128 * (2 * i + p) + 144] for i in range(NOWN)], 0)
        m = dict(shared)
        m["xk"] = np.ascontiguousarray(xk, dtype=np.float32)
        m["xo"] = np.ascontiguousarray(xo, dtype=np.float32)
        m["kvalid"] = np.ascontiguousarray(valid.T.astype(np.float32))
        in_maps.append(m)
    return in_maps, B


_NC_CACHE = {}


def kernel(**inputs):
    in_maps, B = _prep_inputs(inputs)
    if "nc" not in _NC_CACHE:
        _NC_CACHE["nc"] = build(0)
    res = run_bass_kernel_spmd(_NC_CACHE["nc"], in_maps, core_ids=list(range(8)))
    out = np.empty((B, 4096, 1024), np.float32)
    for c in range(8):
        b, p = c // 2, c % 2
        y = np.asarray(res.results[c]["out"], dtype=np.float32).reshape(NOWN, 128, 1024)
        for i in range(NOWN):
            g = 2 * i + p
            out[b, 128 * g:128 * (g + 1)] = y[i]
    return out
```
